# Optimizing a Trainium2 kernel written in Bass

```python
import jax, jax.numpy as jnp
from jax import lax
import numpy as np

D_MODEL = 2048
BATCH = 2
SEQ = 4096
DEPTH = 2
DEC_BATCH = 8
DEC_SEQ = 64
PAST_LEN = 1024

CHUNK = 64
N_MIXERS = 2
N_GLA_LAYERS = (DEPTH + 1) // 2
N_CONV_LAYERS = DEPTH // 2
GLA_HEADS = 4
GLA_DK = D_MODEL // 2
GLA_DV = D_MODEL
GLA_DK_HEAD = GLA_DK // GLA_HEADS
GLA_DV_HEAD = GLA_DV // GLA_HEADS
GATE_RANK = 16
GATE_NORM = 16.0
GLA_IN = 2 * GLA_DK + 2 * GLA_DV + GATE_RANK
CONV_DIM = D_MODEL
CONV_WIDTH = 31
CONV_PAD = CONV_WIDTH - 1
CONV_IN = 3 * CONV_DIM
EPS = 1e-6

kernel_name = 'hybrid_gla_conformer_stream_step'


def _rmsnorm(x, g):
    x32 = x.astype(jnp.float32)
    y = x32 * lax.rsqrt(jnp.mean(x32 * x32, axis=-1, keepdims=True) + EPS)
    return (y * g.astype(jnp.float32)).astype(x.dtype)


def _layernorm(x, g, b):
    x32 = x.astype(jnp.float32)
    mu = jnp.mean(x32, axis=-1, keepdims=True)
    xc = x32 - mu
    y = xc * lax.rsqrt(jnp.mean(xc * xc, axis=-1, keepdims=True) + EPS)
    return (y * g.astype(jnp.float32) + b.astype(jnp.float32)).astype(x.dtype)


def _gla_recurrence(q, k, v, loga, s0):
    B, L, H, _ = q.shape
    dv = v.shape[-1]
    c = CHUNK if L >= CHUNK else L
    n = L // c

    def to_chunks(t):
        return t.reshape(B, n, c, H, t.shape[-1]).transpose(1, 0, 3, 2, 4)

    mask = jnp.tril(jnp.ones((c, c), dtype=bool))

    def step(S, inp):
        qc, kc, vc, ac = inp
        b = jnp.cumsum(ac, axis=-2)
        b_last = b[:, :, -1:, :]
        qe = qc * jnp.exp(b)
        ke = kc * jnp.exp(-b)
        att = jnp.where(mask, jnp.einsum('bhtd,bhsd->bhts', qe, ke), 0.0)
        o = jnp.einsum('bhts,bhsv->bhtv', att, vc) + jnp.einsum('bhtd,bhdv->bhtv', qe, S)
        kd = kc * jnp.exp(b_last - b)
        S = jnp.exp(b_last[:, :, 0, :])[..., None] * S + jnp.einsum('bhsd,bhsv->bhdv', kd, vc)
        return S, o

    S, o = lax.scan(step, s0, (to_chunks(q), to_chunks(k), to_chunks(v), to_chunks(loga)))
    o = o.transpose(1, 0, 3, 2, 4).reshape(B, L, H, dv)
    return o, S


def _gla_mixer(h, w_in, w_a2, b_a, g_norm, w_out, s0):
    B, L, _ = h.shape
    proj = h @ w_in
    q, k, v, r, a1 = jnp.split(
        proj, [GLA_DK, 2 * GLA_DK, 2 * GLA_DK + GLA_DV, 2 * GLA_DK + 2 * GLA_DV], axis=-1)
    loga = jax.nn.log_sigmoid((a1 @ w_a2 + b_a).astype(jnp.float32)) / GATE_NORM
    f32 = jnp.float32
    qh = q.astype(f32).reshape(B, L, GLA_HEADS, GLA_DK_HEAD) * (GLA_DK_HEAD ** -0.5)
    kh = k.astype(f32).reshape(B, L, GLA_HEADS, GLA_DK_HEAD)
    vh = v.astype(f32).reshape(B, L, GLA_HEADS, GLA_DV_HEAD)
    ah = loga.reshape(B, L, GLA_HEADS, GLA_DK_HEAD)
    o, s = _gla_recurrence(qh, kh, vh, ah, s0.astype(f32))
    o = o * lax.rsqrt(jnp.mean(o * o, axis=-1, keepdims=True) + EPS)
    o = o.reshape(B, L, GLA_DV) * g_norm.astype(f32)
    y = (o.astype(h.dtype) * jax.nn.silu(r)) @ w_out
    return y, s.astype(s0.dtype)


def _conv_mixer(h, w_in, b_in, conv_w, conv_b, ln_g, ln_b, w_out, buf):
    proj = h @ w_in + b_in
    a, ga, z = jnp.split(proj, [CONV_DIM, 2 * CONV_DIM], axis=-1)
    u = a * jax.nn.sigmoid(ga)
    u_ext = jnp.concatenate([buf.astype(u.dtype), u], axis=1)
    c = lax.conv_general_dilated(
        u_ext, conv_w[:, None, :], window_strides=(1,), padding='VALID',
        dimension_numbers=('NWC', 'WIO', 'NWC'), feature_group_count=CONV_DIM) + conv_b
    c = _layernorm(c, ln_g, ln_b)
    y = (jax.nn.silu(c) * jax.nn.silu(z)) @ w_out
    return y, u_ext[:, -CONV_PAD:, :]


def _trunk(x, gla_states, conv_bufs, norm_g, final_norm_g, gla_w_in, gla_w_a2, gla_b_a,
           gla_norm_g, gla_w_out, conv_w_in, conv_b_in, conv_w, conv_b, conv_ln_g,
           conv_ln_b, conv_w_out):
    new_gla, new_conv = [], []
    for i in range(DEPTH):
        h = _rmsnorm(x, norm_g[i])
        j = i // N_MIXERS
        if i % N_MIXERS == 0:
            y, s = _gla_mixer(h, gla_w_in[j], gla_w_a2[j], gla_b_a[j], gla_norm_g[j],
                              gla_w_out[j], gla_states[j])
            new_gla.append(s)
        else:
            y, s = _conv_mixer(h, conv_w_in[j], conv_b_in[j], conv_w[j], conv_b[j],
                               conv_ln_g[j], conv_ln_b[j], conv_w_out[j], conv_bufs[j])
            new_conv.append(s)
        x = x + y
    return _rmsnorm(x, final_norm_g), jnp.stack(new_gla), jnp.stack(new_conv)


def setup_inputs(seed: int = 0) -> dict:
    key = jax.random.key(seed)
    ks = jax.random.split(key, 20)
    n = jax.random.normal
    f = jnp.float32
    return {
        'x_prompt': n(ks[0], (BATCH, SEQ, D_MODEL), f),
        'x_sample': n(ks[1], (DEC_BATCH, DEC_SEQ, D_MODEL), f),
        'state_gla': 0.5 * n(ks[2], (N_GLA_LAYERS, DEC_BATCH, GLA_HEADS, GLA_DK_HEAD, GLA_DV_HEAD), f),
        'state_conv': 0.5 * n(ks[3], (N_CONV_LAYERS, DEC_BATCH, CONV_PAD, CONV_DIM), f),
        'norm_g': 1.0 + 0.02 * n(ks[4], (DEPTH, D_MODEL), f),
        'final_norm_g': 1.0 + 0.02 * n(ks[5], (D_MODEL,), f),
        'gla_w_in': n(ks[6], (N_GLA_LAYERS, D_MODEL, GLA_IN), f) * D_MODEL ** -0.5,
        'gla_w_a2': n(ks[7], (N_GLA_LAYERS, GATE_RANK, GLA_DK), f) * GATE_RANK ** -0.5,
        'gla_b_a': 0.1 * n(ks[8], (N_GLA_LAYERS, GLA_DK), f),
        'gla_norm_g': 1.0 + 0.02 * n(ks[9], (N_GLA_LAYERS, GLA_DV), f),
        'gla_w_out': n(ks[10], (N_GLA_LAYERS, GLA_DV, D_MODEL), f) * GLA_DV ** -0.5,
        'conv_w_in': n(ks[11], (N_CONV_LAYERS, D_MODEL, CONV_IN), f) * D_MODEL ** -0.5,
        'conv_b_in': 0.02 * n(ks[12], (N_CONV_LAYERS, CONV_IN), f),
        'conv_w': n(ks[13], (N_CONV_LAYERS, CONV_WIDTH, CONV_DIM), f) * CONV_WIDTH ** -0.5,
        'conv_b': 0.02 * n(ks[14], (N_CONV_LAYERS, CONV_DIM), f),
        'conv_ln_g': 1.0 + 0.02 * n(ks[15], (N_CONV_LAYERS, CONV_DIM), f),
        'conv_ln_b': 0.02 * n(ks[16], (N_CONV_LAYERS, CONV_DIM), f),
        'conv_w_out': n(ks[17], (N_CONV_LAYERS, CONV_DIM, D_MODEL), f) * CONV_DIM ** -0.5,
    }


def reference(x_prompt, x_sample, state_gla, state_conv, norm_g, final_norm_g, gla_w_in,
              gla_w_a2, gla_b_a, gla_norm_g, gla_w_out, conv_w_in, conv_b_in, conv_w, conv_b,
              conv_ln_g, conv_ln_b, conv_w_out):
    gla0 = jnp.zeros((N_GLA_LAYERS, x_prompt.shape[0], GLA_HEADS, GLA_DK_HEAD, GLA_DV_HEAD),
                     state_gla.dtype)
    conv0 = jnp.zeros((N_CONV_LAYERS, x_prompt.shape[0], CONV_PAD, CONV_DIM), state_conv.dtype)
    y_prompt, gla_p, conv_p = _trunk(
        x_prompt, gla0, conv0, norm_g, final_norm_g, gla_w_in, gla_w_a2, gla_b_a, gla_norm_g,
        gla_w_out, conv_w_in, conv_b_in, conv_w, conv_b, conv_ln_g, conv_ln_b, conv_w_out)
    y_sample, gla_s, conv_s = _trunk(
        x_sample, state_gla, state_conv, norm_g, final_norm_g, gla_w_in, gla_w_a2, gla_b_a,
        gla_norm_g, gla_w_out, conv_w_in, conv_b_in, conv_w, conv_b, conv_ln_g, conv_ln_b,
        conv_w_out)
    return (y_prompt, y_sample, gla_p, conv_p, gla_s, conv_s)
```

```python
import numpy as np
import concourse.bass as bass
import concourse.mybir as mybir
from concourse.bass_utils import run_bass_kernel_spmd

F32 = mybir.dt.float32
BF16 = mybir.dt.bfloat16
AF = mybir.ActivationFunctionType
ALU = mybir.AluOpType
AX = mybir.AxisListType

D = 2048
T = 1088
NP_ = 1024
NS = 64
KC = 16
EPS = 1e-6
TBS = [(0, 512), (512, 512), (1024, 64)]
CHUNKS = [(i * 128, 128) for i in range(8)] + [(1024, 64)]
NPASS = 4
XW = NPASS * 1024 + 64
UW = 30 + 1024 + 30 + 64
USO = 30 + 1024


class Res:
    def __init__(self, name):
        self.name = name
        self.w = None
        self.r = {}
        self.lsem = None
        self.ssem = None
        self.lcnt = 0
        self.scnt = 0


class Eng:
    def __init__(self, name, sem):
        self.name = name
        self.sem = sem
        self.cnt = 0
        self.seen = {}
        self.prog = []


class Rec:
    def __init__(self):
        self.calls = []

    def __getattr__(self, name):
        def f(*a, **k):
            self.calls.append((name, a, k))
            return self
        return f


class KB:
    def __init__(self, nc):
        self.nc = nc
        self.sems = []
        self.engs = {}
        self.final = []
        self.dma_toks = {}

    def new_sem(self, name):
        s = self.nc.alloc_semaphore(name)
        self.sems.append(s)
        return len(self.sems) - 1

    def add_eng(self, name):
        self.engs[name] = Eng(name, self.new_sem("e_" + name))

    def _collect(self, e, reads, writes, is_dma):
        waits = {}

        def add(tok, raw):
            if tok is None:
                return
            s, v = tok
            if not is_dma and s == e.sem:
                if e.name == "pe" or not raw:
                    return
            if waits.get(s, 0) < v:
                waits[s] = v

        for r in reads:
            add(r.w, True)
        for w in writes:
            add(w.w, False)
            for s, v in w.r.items():
                add((s, v), False)
        out = []
        for s, v in waits.items():
            if e.seen.get(s, 0) < v:
                e.seen[s] = v
                out.append((s, v))
        return out

    def op(self, eng, fn, reads=(), writes=()):
        e = self.engs[eng]
        waits = self._collect(e, reads, writes, False)
        e.cnt += 1
        tok = (e.sem, e.cnt)
        for r in reads:
            if r.r.get(tok[0], 0) < tok[1]:
                r.r[tok[0]] = tok[1]
        for w in writes:
            w.w = tok
            w.r = {}
        rec = Rec()
        fn(rec)
        e.prog.append((waits, rec.calls, e.sem, 1))
        return tok

    def raw(self, eng, waits, calls):
        e = self.engs[eng]
        e.cnt += 1
        e.prog.append((list(waits), calls, e.sem, 1))
        return (e.sem, e.cnt)

    def dma(self, eng, out_ap, in_ap, reads=(), writes=(), store=False):
        e = self.engs[eng]
        waits = self._collect(e, reads, writes, True)
        if store:
            rs = reads[0]
            if rs.ssem is None:
                rs.ssem = self.new_sem("s_" + rs.name)
            rs.scnt += 16
            tok = (rs.ssem, rs.scnt)
        else:
            ws = writes[0]
            if ws.lsem is None:
                ws.lsem = self.new_sem("l_" + ws.name)
            ws.lcnt += 16
            tok = (ws.lsem, ws.lcnt)
        for r in reads:
            if r.r.get(tok[0], 0) < tok[1]:
                r.r[tok[0]] = tok[1]
        for w in writes:
            w.w = tok
            w.r = {}

        e.prog.append((waits, [("dma_start", (), dict(out=out_ap, in_=in_ap))], tok[0], 16))
        nm = (reads[0] if store else writes[0]).name
        if not (nm.startswith("W") and nm[1:].isdigit()):
            self.dma_toks[tok[0]] = tok[1]
        return tok

    def barrier(self, names=("pe", "act", "dve", "pool")):
        toks = [(self.engs[n].sem, self.engs[n].cnt) for n in names] + list(self.dma_toks.items())
        for n in tuple(names) + ("sp",):
            e = self.engs[n]
            waits = []
            for s, v in toks:
                if s != e.sem and v > 0 and e.seen.get(s, 0) < v:
                    e.seen[s] = v
                    waits.append((s, v))
            if waits:
                e.prog.append((waits, None, None, 0))

    def emit(self, eng, en):
        e = self.engs[eng]
        for waits, calls, sem, inc in e.prog:
            for s, v in waits:
                en.wait_ge(self.sems[s], v)
            if calls is not None:
                ins = None
                for name, a, k in calls:
                    ins = getattr(en, name)(*a, **k)
                ins.then_inc(self.sems[sem], inc)


def build_nc():
    nc = bass.Bass("TRN2", target_bir_lowering=False)
    kb = KB(nc)
    for n in ("pe", "act", "dve", "pool", "sp"):
        kb.add_eng(n)

    def din(name, shape, dt=F32):
        return nc.dram_tensor(name, shape, dt, kind="ExternalInput").ap()

    def dout(name, shape, dt=F32):
        return nc.dram_tensor(name, shape, dt, kind="ExternalOutput").ap()

    xT_d = din("xT", [D, XW])
    w0_d = din("w0", [32, 128, KC, 256])
    w1_d = din("w1", [32, 128, KC, 256])
    wa1_d = din("wa1", [128, KC, 16])
    wa2_d = din("wa2", [16, 1024])
    vecs_d = din("vecs", [128, 664])
    sg_d = din("sg", [128, 8, 512])
    sc_d = din("sc", [128, 16, 30])
    flag_d = din("flag", [128, 1])
    yT_d = dout("yT", [D, T])
    glap_d = dout("glap", [128, 8, 512])
    glas_d = dout("glas", [128, 8, 512])
    cst_d = dout("cst", [128, 16, 60])
    x1_d = nc.dram_tensor("x1stash", [D, T], F32).ap()
    sdram = nc.dram_tensor("sdram", [128, 4096], F32).ap()
    hbt_d = nc.dram_tensor("hbt", [128, KC, 128], BF16).ap()

    def sb(name, shape, dt):
        return nc.alloc_sbuf_tensor("sb_" + name, shape, dt)

    Hb = sb("Hb", [128, KC, T], BF16)
    Vb = sb("Vb", [128, 9, 2048], BF16)
    Gb = sb("Gb", [128, KC, T], BF16)
    Rb = sb("Rb", [128, KC, T], F32)
    Wb = [sb("W%d" % i, [128, KC, 256], BF16) for i in range(3)]
    vecs = sb("vecs", [128, 664], F32)
    ident = sb("ident", [128, 128], BF16)
    ones = sb("ones", [128, 128], BF16)
    trimask = sb("trimask", [128, 128], F32)
    wa1 = sb("wa1", [128, KC, 16], BF16)
    dec = sb("dec", [128, 8, 9], F32)
    nba = sb("nba", [128, 8], F32)
    ust = sb("ust", [128, 16, 60], F32)
    smalls = sb("smalls", [128, 512], F32)
    flag = sb("flag", [128, 1], F32)

    V_NG0, V_NG1, V_FG, V_GNG, V_BA, V_CBIN, V_CB, V_LNG, V_LNB, V_CW = 0, 16, 32, 48, 64, 72, 120, 136, 152, 168

    Rbf = Rb[:].rearrange("p k t -> p (k t)").bitcast(BF16)
    QE = Rbf[:, 0:8 * T].rearrange("p (j t) -> p j t", j=8)
    KE = Rbf[:, 8 * T:16 * T].rearrange("p (j t) -> p j t", j=8)
    KDT = Rbf[:, 16 * T:16 * T + 9 * 1024].rearrange("p (c d) -> p c d", c=9)
    o_tmp = 16 * T + 9 * 1024
    Rf = Rb[:].rearrange("p k t -> p (k t)")
    f_off = (o_tmp + 1) // 2
    EB = Rf[:, f_off:f_off + T]
    EMB = Rf[:, f_off + T:f_off + 2 * T]
    ED = Rf[:, f_off + 2 * T:f_off + 3 * T]
    KDTMP = Rf[:, f_off + 3 * T:f_off + 3 * T + T // 2].bitcast(BF16)
    assert f_off + 3 * T + T // 2 <= KC * T
    Vf = Vb[:].rearrange("p c d -> p (c d)").bitcast(F32)
    TMP1 = Vf[:, 0:T]
    CS = Vf[:, T:2 * T]
    RESET = Vf[:, 2 * T:3 * T]
    RSTD = Vf[:, 3 * T:4 * T]
    identf = Vf[:, 6 * T:6 * T + 128]
    wa2 = Gb[0:16, 14, 0:1024]
    a1T = Gb[0:16, 15, :]
    KDTMP2 = Vf[:, 5 * T:5 * T + T // 2].bitcast(BF16)
    SQT = [Vf[:, 4 * T + i * (T // 2):4 * T + (i + 1) * (T // 2)].bitcast(BF16) for i in range(2)]
    Hf = Hb[:].rearrange("p k t -> p (k t)").bitcast(F32)
    S32 = Hf[:, 0:4096].rearrange("p (j v) -> p j v", j=8)
    SBF = Hf[:, 4096:6144].bitcast(BF16).rearrange("p (j v) -> p j v", j=8)
    AM = [Hf[:, 6144 + i * 64:6144 + (i + 1) * 64].bitcast(BF16) for i in range(2)]
    SQO = [Hf[:, 6272 + i * 256:6272 + (i + 1) * 256].bitcast(BF16) for i in range(2)]
    RSO = [Hf[:, 6784 + i * 128:6784 + (i + 1) * 128] for i in range(2)]
    OT = [Hf[:, 7040 + i * 512:7040 + (i + 1) * 512] for i in range(2)]
    STG = Hf[:, 0:0]
    XST = [Hf[:, 8064 - 0 + 0:8064] for _ in range(0)]

    ps = [nc.alloc_psum_tensor("ps%d" % i, [128, 512], F32) for i in range(8)]
    psr = [Res("ps%d" % i) for i in range(8)]
    bank_ctr = [0]

    def bank():
        b = bank_ctr[0] % 8
        bank_ctr[0] += 1
        return b

    r_vecs = Res("consts")
    r_H = [Res("H%d" % k) for k in range(KC)]
    r_V = [Res("V%d" % c) for c in range(9)]
    r_G = [Res("G%d" % k) for k in range(KC)]
    r_R = [Res("R%d" % k) for k in range(KC)]
    r_W = [Res("W%d" % i) for i in range(3)]
    r_misc = {}

    rl_cache = {}

    def RL(name, n):
        if name not in rl_cache:
            rl_cache[name] = [Res("%s%d" % (name, i)) for i in range(n)]
        return rl_cache[name]

    def R_(name):
        if name not in r_misc:
            r_misc[name] = Res(name)
        return r_misc[name]

    kb.dma("sp", vecs[:], vecs_d[:, :], writes=[r_vecs])
    kb.dma("sp", flag[:], flag_d[:, :], writes=[r_vecs])
    kb.dma("pool", wa1[:], wa1_d[:, :, :], writes=[R_("wa1")])
    r_c = R_("cgen")
    kb.op("pool", lambda g: g.memset(identf, 0.0), writes=[r_c])
    kb.op("pool", lambda g: g.affine_select(out=identf, in_=identf, compare_op=ALU.not_equal, fill=1.0,
                                            base=0, pattern=[[-1, 128]], channel_multiplier=1),
          reads=[r_c], writes=[r_c])
    kb.op("pool", lambda g: g.tensor_copy(out=ident[:], in_=identf), reads=[r_c], writes=[R_("ident")])
    kb.op("pool", lambda g: g.memset(ones[:], 1.0), writes=[R_("ones")])
    r_tm = R_("trimask")
    kb.op("pool", lambda g: g.memset(trimask[:], 1.0), writes=[r_tm])
    kb.op("pool", lambda g: g.affine_select(out=trimask[:], in_=trimask[:], compare_op=ALU.is_ge, fill=0.0,
                                            base=0, pattern=[[1, 128]], channel_multiplier=-1),
          reads=[r_tm], writes=[r_tm])
    kb.op("dve", lambda v: v.tensor_scalar(out=nba[:], in0=vecs[:, V_BA:V_BA + 8], scalar1=-1.0, scalar2=None,
                                           op0=ALU.mult), reads=[r_vecs], writes=[R_("nba")])

    wq = {"n": 0}

    def wload(src_ap):
        i = wq["n"] % 3
        wq["n"] += 1
        kb.dma("pool", Wb[i][:], src_ap, writes=[r_W[i]])
        return i

    MODES = ["state", "state", "tou", "main"]
    NSLOT = {"state": 16, "tou": 32, "main": 64}
    wsched = []
    wmap = {}
    for _p in range(NPASS):
        for _l in range(NSLOT[MODES[_p]]):
            wmap[(_p, _l)] = len(wsched)
            wsched.append(w0_d[_l] if _l < 32 else w1_d[_l - 32])
    wstate = {"issued": 0, "slots": []}

    def wprefetch(upto):
        while wstate["issued"] < min(upto, len(wsched)):
            wstate["slots"].append(wload(wsched[wstate["issued"]]))
            wstate["issued"] += 1

    def wslot(idx):
        wprefetch(idx + 3)
        return wstate["slots"][idx]

    def rsqrt_ip(ap, res):
        kb.op("act", lambda a: a.activation(out=ap, in_=ap, func=AF.Ln), reads=[res], writes=[res])
        kb.op("act", lambda a: a.activation(out=ap, in_=ap, func=AF.Exp, scale=-0.5), reads=[res], writes=[res])

    def rms_prep(xsrc, xres, gcol, tbs=None):
        tbs = tbs or TBS
        lo = min(t0 for t0, tn in tbs)
        hi = max(t0 + tn for t0, tn in tbs)
        bks = [bank() for _ in tbs]
        for k in range(KC):
            sq = SQT[k % 2]
            rsq = R_("sqt%d" % (k % 2))
            kb.op("act", lambda a, k=k, sq=sq: a.activation(out=sq[:, lo:hi], in_=xsrc[:, k, lo:hi], func=AF.Square),
                  reads=[xres[k]], writes=[rsq])

            def mm(pe, k=k, sq=sq):
                ins = None
                for bi, (t0, tn) in enumerate(tbs):
                    ins = pe.matmul(ps[bks[bi]][:, 0:tn], lhsT=ones[:], rhs=sq[:, t0:t0 + tn], start=(k == 0),
                                    stop=(k == KC - 1))
                return ins
            kb.op("pe", mm, reads=[rsq, R_("ones")], writes=[psr[b] for b in bks])
        r_rstd = R_("rstd")
        for bi, (t0, tn) in enumerate(tbs):
            kb.op("dve", lambda v, bi=bi, t0=t0, tn=tn: v.tensor_scalar(
                out=RSTD[:, t0:t0 + tn], in0=ps[bks[bi]][:, 0:tn], scalar1=1.0 / D, scalar2=EPS, op0=ALU.mult,
                op1=ALU.add), reads=[psr[bks[bi]]], writes=[r_rstd])
        rsqrt_ip(RSTD[:, lo:hi], r_rstd)
        for k in range(KC):
            kb.op("dve", lambda v, k=k: v.scalar_tensor_tensor(
                out=Hb[:, k, lo:hi], in0=xsrc[:, k, lo:hi], scalar=vecs[:, gcol + k:gcol + k + 1], in1=RSTD[:, lo:hi],
                op0=ALU.mult, op1=ALU.mult), reads=[xres[k], r_rstd, r_vecs], writes=[r_H[k]])

    cur = {"tbs": TBS}
    TB_TAIL = [(NP_ - 128, 128)]

    def proj_ws(slot, col0, evac, tbs=None):
        tbs = tbs or TBS
        cur["tbs"] = tbs
        bks = [bank() for _ in tbs]

        def mm(pe):
            ins = None
            for k in range(KC):
                for bi, (t0, tn) in enumerate(tbs):
                    ins = pe.matmul(ps[bks[bi]][:, 0:tn], lhsT=Wb[slot][:, k, col0:col0 + 128],
                                    rhs=Hb[:, k, t0:t0 + tn], start=(k == 0), stop=(k == KC - 1))
            return ins
        kb.op("pe", mm, reads=[r_W[slot]] + r_H, writes=[psr[b] for b in bks])
        evac(bks)
        cur["tbs"] = TBS

    def run_pass(PS):
        mode = MODES[PS]
        TBX = TB_TAIL if mode == "tou" else TBS
        TBA = TBS if mode == "main" else TBS[0:2]
        kb.dma("pool", wa2, wa2_d[:, :], writes=[R_("wa2"), r_G[14]])
        for k in range(KC):
            kb.dma("sp", Rb[:, k, 0:NP_], xT_d[k * 128:(k + 1) * 128, PS * NP_:(PS + 1) * NP_], writes=[r_R[k]])
            kb.dma("sp", Rb[:, k, NP_:T], xT_d[k * 128:(k + 1) * 128, NPASS * NP_:XW], writes=[r_R[k]])
        r_reset = R_("reset")
        kb.op("pool", lambda g: g.memset(RESET, 1.0), writes=[r_reset])
        kb.op("pool", lambda g: g.memset(RESET.rearrange("p (c t) -> p c t", t=64)[:, 0:17:2, 0:1], 0.0),
              reads=[r_reset], writes=[r_reset])
        rms_prep(Rb, r_R, V_NG0, TBA)
        kb.barrier()

        bks = [bank() for _ in TBA]

        def mm_a1(pe):
            ins = None
            for k in range(KC):
                for bi, (t0, tn) in enumerate(TBA):
                    ins = pe.matmul(ps[bks[bi]][0:16, 0:tn], lhsT=wa1[:, k, :], rhs=Hb[:, k, t0:t0 + tn],
                                    start=(k == 0), stop=(k == KC - 1))
            return ins
        kb.op("pe", mm_a1, reads=[R_("wa1")] + r_H, writes=[psr[b] for b in bks])
        r_a1 = R_("a1T")
        for bi, (t0, tn) in enumerate(TBA):
            kb.op("act", lambda a, bi=bi, t0=t0, tn=tn: a.activation(out=a1T[:, t0:t0 + tn], in_=ps[bks[bi]][0:16, 0:tn],
                                                                    func=AF.Copy),
                  reads=[psr[bks[bi]]], writes=[r_a1])

        r_tmp1, r_cs, r_eb, r_emb, r_ed, r_kdtmp = (R_(n) for n in ("tmp1", "cs", "eb", "emb", "ed", "kdtmp"))
        r_dec = R_("dec")
        r_QE = RL("QE", 8)
        r_KE = RL("KE", 8)
        r_KDT = RL("KDT", 8)
        KDB = [KDTMP, KDTMP2]
        r_kdb = [R_("kdb0"), R_("kdb1")]
        pending_tr = [None]
        for j in range(8):
            bks = [bank() for _ in TBS]

            def mm_z(pe, j=j, bks=bks):
                ins = None
                for bi, (t0, tn) in enumerate(TBS):
                    ins = pe.matmul(ps[bks[bi]][:, 0:tn], lhsT=wa2[:, j * 128:(j + 1) * 128], rhs=a1T[:, t0:t0 + tn],
                                    start=True, stop=True)
                return ins
            kb.op("pe", mm_z, reads=[R_("wa2"), r_a1], writes=[psr[b] for b in bks])
            for bi, (t0, tn) in enumerate(TBS):
                kb.op("act", lambda a, j=j, bi=bi, t0=t0, tn=tn, bks=bks: a.activation(
                    out=TMP1[:, t0:t0 + tn], in_=ps[bks[bi]][:, 0:tn], func=AF.Exp, bias=nba[:, j:j + 1], scale=-1.0),
                    reads=[psr[bks[bi]], R_("nba")], writes=[r_tmp1])
            kb.op("act", lambda a: a.activation(out=TMP1, in_=TMP1, func=AF.Ln, bias=1.0, scale=1.0),
                  reads=[r_tmp1], writes=[r_tmp1])
            kb.op("dve", lambda v: v.tensor_tensor_scan(out=CS, data0=RESET, data1=TMP1, initial=0.0, op0=ALU.mult,
                                                        op1=ALU.add), reads=[r_tmp1, r_reset], writes=[r_cs])
            if mode != "state":
                kb.op("act", lambda a: a.activation(out=EB, in_=CS, func=AF.Exp, scale=-1.0 / 16), reads=[r_cs], writes=[r_eb])
                kb.op("act", lambda a: a.activation(out=EMB, in_=CS, func=AF.Exp, scale=1.0 / 16), reads=[r_cs], writes=[r_emb])
            kb.op("act", lambda a, j=j: a.activation(out=dec[:, j, 0:8], in_=CS[:, 127:1024:128], func=AF.Exp,
                                                     scale=-1.0 / 16), reads=[r_cs], writes=[r_dec])
            if mode == "main":
                kb.op("act", lambda a, j=j: a.activation(out=dec[:, j, 8:9], in_=CS[:, 1087:1088], func=AF.Exp,
                                                         scale=-1.0 / 16), reads=[r_cs], writes=[r_dec])
            kb.op("dve", lambda v: v.tensor_tensor(
                out=ED[:, 0:1024].rearrange("p (c t) -> p c t", c=8), in0=CS[:, 0:1024].rearrange("p (c t) -> p c t", c=8),
                in1=CS[:, 127:1024:128].unsqueeze(2).to_broadcast([128, 8, 128]), op=ALU.subtract),
                reads=[r_cs], writes=[r_ed])
            if mode == "main":
                kb.op("dve", lambda v: v.tensor_scalar(out=ED[:, 1024:1088], in0=CS[:, 1024:1088], scalar1=CS[:, 1087:1088],
                                                       scalar2=None, op0=ALU.subtract), reads=[r_cs], writes=[r_ed])
            kb.op("act", lambda a: a.activation(out=ED, in_=ED, func=AF.Exp, scale=1.0 / 16), reads=[r_ed], writes=[r_ed])
            slot = wslot(wmap[(PS, j)])

            def evac_k(bks, j=j):
                for bi, (t0, tn) in enumerate(cur["tbs"]):
                    if mode != "state":
                        kb.op("dve", lambda v, bi=bi, t0=t0, tn=tn: v.tensor_tensor(
                            out=KE[:, j, t0:t0 + tn], in0=ps[bks[bi]][:, 0:tn], in1=EMB[:, t0:t0 + tn], op=ALU.mult),
                            reads=[psr[bks[bi]], r_emb], writes=[r_KE[j]])
                    kb.op("dve", lambda v, bi=bi, t0=t0, tn=tn: v.tensor_tensor(
                        out=KDB[j % 2][:, t0:t0 + tn], in0=ps[bks[bi]][:, 0:tn], in1=ED[:, t0:t0 + tn], op=ALU.mult),
                        reads=[psr[bks[bi]], r_ed], writes=[r_kdb[j % 2]])
            proj_ws(slot, 0, evac_k, TBA)
            def transposes(j=j):
                nchk = 9 if mode == "main" else 8
                for c0 in range(0, nchk, 4):
                    cs_ = list(range(c0, min(c0 + 4, nchk)))
                    b = bank()
                    pbf = ps[b][:].bitcast(BF16)

                    def tr(pe, cs_=cs_, pbf=pbf):
                        ins = None
                        for ii, c in enumerate(cs_):
                            t0, tn = CHUNKS[c]
                            ins = pe.transpose(out=pbf[0:tn, ii * 128:(ii + 1) * 128], in_=KDB[j % 2][:, t0:t0 + tn],
                                               identity=ident[:])
                        return ins
                    kb.op("pe", tr, reads=[r_kdb[j % 2], R_("ident")], writes=[psr[b]])
                    for ii, c in enumerate(cs_):
                        tn = CHUNKS[c][1]
                        kb.op("act", lambda a, ii=ii, c=c, tn=tn, pbf=pbf, b=b: a.activation(
                            out=KDT[0:tn, c, j * 128:(j + 1) * 128], in_=pbf[0:tn, ii * 128:(ii + 1) * 128], func=AF.Copy),
                            reads=[psr[b]], writes=[r_KDT[j]])

            def evac_q(bks, j=j):
                for bi, (t0, tn) in enumerate(cur["tbs"]):
                    kb.op("dve", lambda v, bi=bi, t0=t0, tn=tn: v.scalar_tensor_tensor(
                        out=QE[:, j, t0:t0 + tn], in0=ps[bks[bi]][:, 0:tn], scalar=0.0625, in1=EB[:, t0:t0 + tn],
                        op0=ALU.mult, op1=ALU.mult), reads=[psr[bks[bi]], r_eb], writes=[r_QE[j]])
            if mode != "state":
                proj_ws(slot, 128, evac_q, TBX)
            if pending_tr[0] is not None:
                pending_tr[0]()
            pending_tr[0] = transposes
        pending_tr[0]()

        kb.barrier()
        for s in range(8):
            slot = wslot(wmap[(PS, 8 + s)])
            for c, (t0, tn) in enumerate(CHUNKS):
                if mode != "main" and c == 8:
                    continue
                b = bank()

                def mm_v(pe, t0=t0, tn=tn, b=b, slot=slot):
                    ins = None
                    for k in range(KC):
                        ins = pe.matmul(ps[b][0:tn, 0:256], lhsT=Hb[:, k, t0:t0 + tn], rhs=Wb[slot][:, k, :],
                                        start=(k == 0), stop=(k == KC - 1))
                    return ins
                kb.op("pe", mm_v, reads=[r_W[slot]] + r_H, writes=[psr[b]])
                eng = "act" if c % 2 == 0 else "dve"
                if eng == "act":
                    kb.op("act", lambda a, c=c, tn=tn, b=b, s=s: a.activation(
                        out=Vb[0:tn, c, s * 256:(s + 1) * 256], in_=ps[b][0:tn, 0:256], func=AF.Copy),
                        reads=[psr[b]], writes=[r_V[c]])
                else:
                    kb.op("dve", lambda v, c=c, tn=tn, b=b, s=s: v.tensor_copy(
                        out=Vb[0:tn, c, s * 256:(s + 1) * 256], in_=ps[b][0:tn, 0:256]),
                        reads=[psr[b]], writes=[r_V[c]])
        for s in range(8 if mode != "state" else 0):
            slot = wslot(wmap[(PS, 16 + s)])
            for half in range(2):
                oc = 2 * s + half

                def evac_r(bks, oc=oc):
                    for bi, (t0, tn) in enumerate(cur["tbs"]):
                        kb.op("act", lambda a, bi=bi, t0=t0, tn=tn: a.activation(
                            out=Gb[:, oc, t0:t0 + tn], in_=ps[bks[bi]][:, 0:tn], func=AF.Silu),
                            reads=[psr[bks[bi]]], writes=[r_G[oc]])
                    glo = min(t0 for t0, tn in cur["tbs"])
                    ghi = max(t0 + tn for t0, tn in cur["tbs"])
                    kb.op("dve", lambda g: g.tensor_scalar(out=Gb[:, oc, glo:ghi], in0=Gb[:, oc, glo:ghi],
                                                            scalar1=vecs[:, V_GNG + oc:V_GNG + oc + 1], scalar2=None,
                                                            op0=ALU.mult), reads=[r_G[oc], r_vecs], writes=[r_G[oc]])
                proj_ws(slot, half * 128, evac_r, TBX)
        kb.barrier()

        r_S = RL("S", 8)
        r_SB = RL("SB", 8)
        r_sd = R_("sdram")
        if PS == 0:
            kb.op("pool", lambda g: g.memset(Hf[:, 0:4096], 0.0), writes=r_S)
        else:
            kb.dma("sp", Hf[:, 0:4096], sdram[:, :], reads=[r_sd], writes=r_S + r_H)

        def state_update(c, j, tn, with_bf):
            h = j // 2
            b = bank()
            kb.op("pe", lambda pe: pe.matmul(ps[b][:, :], lhsT=KDT[0:tn, c, j * 128:(j + 1) * 128],
                                             rhs=Vb[0:tn, c, h * 512:(h + 1) * 512], start=True, stop=True),
                  reads=[r_KDT[j], r_V[c]], writes=[psr[b]])
            kb.op("dve", lambda v: v.scalar_tensor_tensor(out=S32[:, j, :], in0=S32[:, j, :], scalar=dec[:, j, c:c + 1],
                                                          in1=ps[b][:, :], op0=ALU.mult, op1=ALU.add),
                  reads=[psr[b], r_S[j], r_dec], writes=[r_S[j]])
            if with_bf:
                kb.op("act", lambda a: a.activation(out=SBF[:, j, :], in_=S32[:, j, :], func=AF.Copy),
                      reads=[r_S[j]], writes=[r_SB[j]])

        for j in range(8 if mode == "main" else 0):
            kb.op("act", lambda a, j=j: a.activation(out=SBF[:, j, :], in_=S32[:, j, :], func=AF.Copy), reads=[r_S[j]],
                  writes=[r_SB[j]])

        r_am = [R_("am0"), R_("am1")]
        r_sqo = [R_("sqo0"), R_("sqo1")]
        r_rso = [R_("rso0"), R_("rso1")]
        r_ot = [R_("ot0"), R_("ot1")]
        it = [0]

        def chunk_head(c, h):
            t0, tn = CHUNKS[c]
            q = it[0] % 2
            it[0] += 1
            bA = bank()

            def mmA(pe):
                ins = None
                for dd in range(2):
                    j = 2 * h + dd
                    ins = pe.matmul(ps[bA][0:tn, 0:tn], lhsT=KE[:, j, t0:t0 + tn], rhs=QE[:, j, t0:t0 + tn],
                                    start=(dd == 0), stop=(dd == 1))
                return ins
            kb.op("pe", mmA, reads=[r_KE[2 * h], r_KE[2 * h + 1], r_QE[2 * h], r_QE[2 * h + 1]], writes=[psr[bA]])
            kb.op("dve", lambda v: v.tensor_tensor(out=AM[q][0:tn, 0:tn], in0=ps[bA][0:tn, 0:tn], in1=trimask[0:tn, 0:tn],
                                                   op=ALU.mult), reads=[psr[bA], r_tm], writes=[r_am[q]])
            bO = bank()
            po = ps[bO][:].rearrange("p (m t) -> p m t", m=4)

            def mmO(pe):
                ins = None
                for m in range(4):
                    ins = pe.matmul(po[:, m, 0:tn], lhsT=Vb[0:tn, c, h * 512 + m * 128:h * 512 + (m + 1) * 128],
                                    rhs=AM[q][0:tn, 0:tn], start=True, stop=False)
                    for dd in range(2):
                        j = 2 * h + dd
                        ins = pe.matmul(po[:, m, 0:tn], lhsT=SBF[:, j, m * 128:(m + 1) * 128], rhs=QE[:, j, t0:t0 + tn],
                                        start=False, stop=(dd == 1))
                return ins
            kb.op("pe", mmO, reads=[r_V[c], r_am[q], r_SB[2 * h], r_SB[2 * h + 1], r_QE[2 * h], r_QE[2 * h + 1]],
                  writes=[psr[bO]])
            sqv = SQO[q].rearrange("p (m t) -> p m t", m=4)
            kb.op("act", lambda a: a.activation(out=sqv[:, :, 0:tn], in_=po[:, :, 0:tn], func=AF.Square),
                  reads=[psr[bO]], writes=[r_sqo[q]])
            bS = bank()

            def mmS(pe):
                ins = None
                for m in range(4):
                    ins = pe.matmul(ps[bS][:, 0:tn], lhsT=ones[:], rhs=sqv[:, m, 0:tn], start=(m == 0), stop=(m == 3))
                return ins
            kb.op("pe", mmS, reads=[r_sqo[q], R_("ones")], writes=[psr[bS]])
            kb.op("dve", lambda v: v.tensor_scalar(out=RSO[q][:, 0:tn], in0=ps[bS][:, 0:tn], scalar1=1.0 / 512, scalar2=EPS,
                                                   op0=ALU.mult, op1=ALU.add), reads=[psr[bS]], writes=[r_rso[q]])
            rsqrt_ip(RSO[q][:, 0:tn], r_rso[q])
            otv = OT[q].rearrange("p (m t) -> p m t", m=4)
            kb.op("dve", lambda v: v.tensor_tensor(out=otv[:, :, 0:tn], in0=po[:, :, 0:tn],
                                                   in1=RSO[q][:, 0:tn].unsqueeze(1).to_broadcast([128, 4, tn]), op=ALU.mult),
                  reads=[psr[bO], r_rso[q]], writes=[r_ot[q]])
            gr = [r_G[4 * h + m] for m in range(4)]
            kb.op("pool", lambda g: g.tensor_tensor(out=Gb[:, 4 * h:4 * h + 4, t0:t0 + tn], in0=Gb[:, 4 * h:4 * h + 4, t0:t0 + tn],
                                                    in1=otv[:, :, 0:tn], op=ALU.mult), reads=[r_ot[q]] + gr, writes=gr)
            for dd in range(2):
                state_update(c, 2 * h + dd, tn, True)

        for c in range(8):
            if mode == "state" or (mode == "tou" and c < 7):
                for j in range(8):
                    state_update(c, j, 128, False)
            else:
                if mode == "tou":
                    for j in range(8):
                        kb.op("act", lambda a, j=j: a.activation(out=SBF[:, j, :], in_=S32[:, j, :], func=AF.Copy),
                              reads=[r_S[j]], writes=[r_SB[j]])
                for h in range(4):
                    chunk_head(c, h)
        if PS == NPASS - 1:
            kb.final.append(kb.dma("sp", glap_d.rearrange("p j v -> p (j v)"), Hf[:, 0:4096], reads=r_S + r_H, writes=[R_("glap")],
                                   store=True))
        else:
            kb.dma("sp", sdram[:, :], Hf[:, 0:4096], reads=r_S + r_H, writes=[r_sd], store=True)
        if mode == "state":
            kb.barrier()
            return
        if mode == "main":
            kb.dma("sp", Hf[:, 0:4096], sg_d.rearrange("p j v -> p (j v)"), writes=r_S)
            for j in range(8):
                kb.op("act", lambda a, j=j: a.activation(out=SBF[:, j, :], in_=S32[:, j, :], func=AF.Copy), reads=[r_S[j]],
                      writes=[r_SB[j]])
            for h in range(4):
                chunk_head(8, h)
            kb.final.append(kb.dma("sp", glas_d.rearrange("p j v -> p (j v)"), Hf[:, 0:4096], reads=r_S + r_H, writes=[R_("glas")],
                                   store=True))
        kb.barrier()

        XSTG = [Vf[:, i * T:(i + 1) * T] for i in range(2)]
        r_xstg = [R_("xstg0"), R_("xstg1")]
        r_x1d = RL("x1d", KC)

        def outproj(wbase, xsrc_d, dst_res, stash, Gsrc, r_Gsrc, XSTG, r_xstg, alias, tbs=None, do_store=True):
            tbs = tbs or TBS
            for s in range(8):
                slot = wslot(wmap[(PS, wbase + s)])
                for half in range(2):
                    oc = 2 * s + half
                    q = oc % 2
                    if stash:
                        kb.dma("sp", XSTG[q][:, 0:NP_], xsrc_d[oc * 128:(oc + 1) * 128, PS * NP_:(PS + 1) * NP_],
                               writes=[r_xstg[q]] + alias)
                        kb.dma("sp", XSTG[q][:, NP_:T], xsrc_d[oc * 128:(oc + 1) * 128, NPASS * NP_:XW], writes=[r_xstg[q]])
                    else:
                        kb.dma("sp", XSTG[q], xsrc_d[oc * 128:(oc + 1) * 128, :], reads=[r_x1d[oc]],
                               writes=[r_xstg[q]] + alias)
                    bks = [bank() for _ in tbs]

                    def mm(pe, slot=slot, half=half, bks=bks):
                        ins = None
                        for k in range(KC):
                            for bi, (t0, tn) in enumerate(tbs):
                                ins = pe.matmul(ps[bks[bi]][:, 0:tn], lhsT=Wb[slot][:, k, half * 128:(half + 1) * 128],
                                                rhs=Gsrc[:, k, t0:t0 + tn], start=(k == 0), stop=(k == KC - 1))
                        return ins
                    kb.op("pe", mm, reads=[r_W[slot]] + r_Gsrc, writes=[psr[b] for b in bks])
                    for bi, (t0, tn) in enumerate(tbs):
                        kb.op("dve", lambda v, bi=bi, t0=t0, tn=tn, bks=bks, oc=oc, q=q: v.tensor_tensor(
                            out=Rb[:, oc, t0:t0 + tn], in0=ps[bks[bi]][:, 0:tn], in1=XSTG[q][:, t0:t0 + tn], op=ALU.add),
                            reads=[psr[bks[bi]], r_xstg[q]], writes=[dst_res[oc]])
                    if stash and do_store:
                        kb.dma("sp", x1_d[oc * 128:(oc + 1) * 128, :], Rb[:, oc, :], reads=[dst_res[oc]], writes=[r_x1d[oc]],
                               store=True)

        outproj(24, xT_d, r_R, True, Gb, r_G, XSTG, r_xstg, r_V, TBX, mode != "tou")
        kb.barrier()

        rms_prep(Rb, r_R, V_NG1, TBX)
        if mode == "tou":
            kb.dma("sp", hbt_d[:, :, :], Hb[:, :, NP_ - 128:NP_], reads=r_H, writes=[R_("hbt")], store=True)
            kb.barrier()
            return
        kb.barrier()
        U = Vb[:].rearrange("p c d -> p (c d)")[:, 0:16 * UW].rearrange("p (k t) -> p k t", k=16)
        r_U = RL("U", 16)
        r_ust = R_("ust")
        SZ = Gb
        r_SZ = r_G
        Rf2 = Rb[:].rearrange("p k t -> p (k t)")
        Gbf = Gb[:].rearrange("p k t -> p (k t)")
        HBT = Gbf[:, 0:KC * 128].rearrange("p (k t) -> p k t", k=KC)
        Gf2 = Gbf.bitcast(F32)
        ATl = Gf2[:, 1024:1152]
        SGTl = Gf2[:, 1152:1280]
        r_hbt, r_atl, r_sgtl = R_("HBT"), R_("ATl"), R_("SGTl")
        kb.dma("sp", HBT, hbt_d[:, :, :], reads=[R_("hbt")], writes=[r_hbt] + r_G)
        A32 = [Rf2[:, (2 * i) * T:(2 * i + 1) * T] for i in range(2)]
        SG32 = [Rf2[:, (2 * i + 1) * T:(2 * i + 2) * T] for i in range(2)]
        r_a32 = [R_("a32_0"), R_("a32_1")]
        r_sg32 = [R_("sg32_0"), R_("sg32_1")]
        for i in range(16):
            slot = wslot(wmap[(PS, 32 + i)])
            q = i % 2

            def evac_a(bks, i=i, q=q):
                for bi, (t0, tn) in enumerate(cur["tbs"]):
                    kb.op("act", lambda a, bi=bi, t0=t0, tn=tn: a.activation(
                        out=A32[q][:, t0:t0 + tn], in_=ps[bks[bi]][:, 0:tn], func=AF.Identity,
                        bias=vecs[:, V_CBIN + i:V_CBIN + i + 1], scale=1.0), reads=[psr[bks[bi]], r_vecs], writes=[r_a32[q]])
            proj_ws(slot, 0, evac_a, TBX)

            def evac_g(bks, i=i, q=q):
                for bi, (t0, tn) in enumerate(cur["tbs"]):
                    kb.op("act", lambda a, bi=bi, t0=t0, tn=tn: a.activation(
                        out=SG32[q][:, t0:t0 + tn], in_=ps[bks[bi]][:, 0:tn], func=AF.Sigmoid,
                        bias=vecs[:, V_CBIN + 16 + i:V_CBIN + 16 + i + 1], scale=1.0),
                        reads=[psr[bks[bi]], r_vecs], writes=[r_sg32[q]])
            proj_ws(slot, 128, evac_g, TBX)
            for half, dst, rdst, fn, bcol in ((0, ATl, r_atl, AF.Identity, V_CBIN + i), (1, SGTl, r_sgtl, AF.Sigmoid, V_CBIN + 16 + i)):
                bt = bank()

                def mmt(pe, slot=slot, half=half, bt=bt):
                    ins = None
                    for k in range(KC):
                        ins = pe.matmul(ps[bt][:, 0:128], lhsT=Wb[slot][:, k, half * 128:(half + 1) * 128], rhs=HBT[:, k, :],
                                        start=(k == 0), stop=(k == KC - 1))
                    return ins
                kb.op("pe", mmt, reads=[r_W[slot], r_hbt, r_G[0], r_G[1]], writes=[psr[bt]])
                kb.op("act", lambda a, bt=bt, dst=dst, fn=fn, bcol=bcol: a.activation(
                    out=dst, in_=ps[bt][:, 0:128], func=fn, bias=vecs[:, bcol:bcol + 1], scale=1.0),
                    reads=[psr[bt], r_vecs], writes=[rdst])
            kb.op("dve", lambda v, i=i: v.scalar_tensor_tensor(out=U[:, i, 0:30], in0=ATl[:, 98:128], scalar=flag[:, 0:1],
                                                              in1=SGTl[:, 98:128], op0=ALU.mult, op1=ALU.mult),
                  reads=[r_atl, r_sgtl, r_vecs, r_G[1], r_G[2]], writes=[r_U[i]])
            if mode == "main":
                kb.op("dve", lambda v, i=i, q=q: v.tensor_tensor(out=U[:, i, 30:30 + NP_], in0=A32[q][:, 0:NP_],
                                                                in1=SG32[q][:, 0:NP_], op=ALU.mult),
                      reads=[r_a32[q], r_sg32[q]], writes=[r_U[i]])
                kb.op("dve", lambda v, i=i, q=q: v.tensor_tensor(out=U[:, i, USO + 30:USO + 30 + NS], in0=A32[q][:, NP_:T],
                                                                in1=SG32[q][:, NP_:T], op=ALU.mult),
                      reads=[r_a32[q], r_sg32[q]], writes=[r_U[i]])
            kb.op("dve", lambda v, i=i, q=q: v.tensor_tensor(out=ust[:, i, 0:30], in0=A32[q][:, NP_ - 30:NP_],
                                                            in1=SG32[q][:, NP_ - 30:NP_], op=ALU.mult),
                  reads=[r_a32[q], r_sg32[q]], writes=[r_ust])
            if mode == "main":
                kb.op("dve", lambda v, i=i, q=q: v.tensor_tensor(out=ust[:, i, 30:60], in0=A32[q][:, T - 30:T],
                                                                in1=SG32[q][:, T - 30:T], op=ALU.mult),
                      reads=[r_a32[q], r_sg32[q]], writes=[r_ust])
        if mode == "tou":
            kb.barrier()
            return
        kb.final.append(kb.dma("sp", cst_d.rearrange("p k r -> p (k r)"), ust[:].rearrange("p k r -> p (k r)"),
                               reads=[r_ust], writes=[R_("cst")], store=True))
        for s in range(8):
            slot = wslot(wmap[(PS, 48 + s)])
            for half in range(2):
                oc = 2 * s + half

                def evac_z(bks, oc=oc):
                    for bi, (t0, tn) in enumerate(cur["tbs"]):
                        kb.op("act", lambda a, bi=bi, t0=t0, tn=tn: a.activation(
                            out=SZ[:, oc, t0:t0 + tn], in_=ps[bks[bi]][:, 0:tn], func=AF.Silu,
                            bias=vecs[:, V_CBIN + 32 + oc:V_CBIN + 32 + oc + 1], scale=1.0),
                            reads=[psr[bks[bi]], r_vecs], writes=[r_SZ[oc]])
                proj_ws(slot, half * 128, evac_z)
        kb.barrier()
        SCS = Rf2[:, 9 * T:9 * T + 480]
        r_scs = R_("scs")
        kb.dma("sp", SCS, sc_d.rearrange("p k r -> p (k r)"), writes=[r_scs, r_R[9]])
        kb.op("dve", lambda v: v.tensor_copy(out=U[:, :, USO:USO + 30], in_=SCS.rearrange("p (k r) -> p k r", k=16)),
              reads=[r_scs], writes=r_U)
        kb.barrier()
        Hbf = Hb[:].rearrange("p k t -> p (k t)")
        DG = [Hbf[:, q * 31 * 128:(q + 1) * 31 * 128].rearrange("p (j m) -> p j m", j=31) for q in range(2)]
        r_dg = [RL("dg0_", 31), RL("dg1_", 31)]
        CW = vecs[:, V_CW:V_CW + 496].rearrange("p (i j) -> p i j", i=16)
        CONVB = [(0, 512, 0), (512, 512, 512), (USO, 64, 1024)]
        def dg_build(i):
            q = i % 2
            for j in range(31):
                if j % 2 == 0:
                    kb.op("act", lambda a, i=i, j=j, q=q: a.activation(out=DG[q][:, j, :], in_=ident[:], func=AF.Copy,
                                                                      scale=CW[:, i, j:j + 1]),
                          reads=[R_("ident"), r_vecs], writes=[r_dg[q][j]])
                else:
                    kb.op("dve", lambda g, i=i, j=j, q=q: g.tensor_scalar(out=DG[q][:, j, :], in0=ident[:],
                                                                         scalar1=CW[:, i, j:j + 1], scalar2=None, op0=ALU.mult),
                          reads=[R_("ident"), r_vecs], writes=[r_dg[q][j]])

        dg_build(0)
        for i in range(16):
            q = i % 2
            if i + 1 < 16:
                dg_build(i + 1)
            bks = [bank() for _ in CONVB]

            def mmc(pe, i=i, q=q, bks=bks):
                ins = None
                for j in range(31):
                    for bi, (u0, n_, o0) in enumerate(CONVB):
                        ins = pe.matmul(ps[bks[bi]][:, 0:n_], lhsT=DG[q][:, j, :], rhs=U[:, i, u0 + j:u0 + j + n_],
                                        start=(j == 0), stop=(j == 30))
                return ins
            kb.op("pe", mmc, reads=r_dg[q] + [r_U[i]], writes=[psr[b] for b in bks])
            for bi, (u0, n_, o0) in enumerate(CONVB):
                kb.op("act", lambda a, i=i, bi=bi, n_=n_, o0=o0, bks=bks: a.activation(
                    out=Rb[:, i, o0:o0 + n_], in_=ps[bks[bi]][:, 0:n_], func=AF.Identity, bias=vecs[:, V_CB + i:V_CB + i + 1],
                    scale=1.0), reads=[psr[bks[bi]], r_vecs], writes=[r_R[i]])
        kb.barrier()
        CBT = [Hf[:, 0 + q * 544:(q + 1) * 544].bitcast(BF16) for q in range(2)]
        SQ1 = [Hf[:, 1088 + q * 544:1088 + (q + 1) * 544].bitcast(BF16) for q in range(2)]
        MU = Hf[:, 2176:2176 + T]
        RS1 = Hf[:, 3264:3264 + T]
        MT = [Hf[:, 4352 + q * 544:4352 + (q + 1) * 544].bitcast(BF16) for q in range(2)]
        r_cbt, r_sq1, r_mt = [R_("cbt0"), R_("cbt1")], [R_("sq10"), R_("sq11")], [R_("mt0"), R_("mt1")]
        r_mu, r_rs1 = R_("mu"), R_("rs1")
        bk1 = [bank() for _ in TBS]
        bk2 = [bank() for _ in TBS]
        for i in range(16):
            q = i % 2
            kb.op("act", lambda a, i=i, q=q: a.activation(out=CBT[q], in_=Rb[:, i, :], func=AF.Copy), reads=[r_R[i]],
                  writes=[r_cbt[q]])
            kb.op("act", lambda a, i=i, q=q: a.activation(out=SQ1[q], in_=Rb[:, i, :], func=AF.Square), reads=[r_R[i]],
                  writes=[r_sq1[q]])

            def mmst(pe, i=i, q=q):
                ins = None
                for bi, (t0, tn) in enumerate(TBS):
                    ins = pe.matmul(ps[bk1[bi]][:, 0:tn], lhsT=ones[:], rhs=CBT[q][:, t0:t0 + tn], start=(i == 0), stop=(i == 15))
                    ins = pe.matmul(ps[bk2[bi]][:, 0:tn], lhsT=ones[:], rhs=SQ1[q][:, t0:t0 + tn], start=(i == 0), stop=(i == 15))
                return ins
            kb.op("pe", mmst, reads=[r_cbt[q], r_sq1[q], R_("ones")], writes=[psr[b] for b in bk1 + bk2])
        for bi, (t0, tn) in enumerate(TBS):
            kb.op("dve", lambda v, bi=bi, t0=t0, tn=tn: v.tensor_scalar(out=MU[:, t0:t0 + tn], in0=ps[bk1[bi]][:, 0:tn],
                                                                       scalar1=1.0 / D, scalar2=None, op0=ALU.mult),
                  reads=[psr[bk1[bi]]], writes=[r_mu])
            kb.op("dve", lambda v, bi=bi, t0=t0, tn=tn: v.tensor_scalar(out=RS1[:, t0:t0 + tn], in0=ps[bk2[bi]][:, 0:tn],
                                                                       scalar1=1.0 / D, scalar2=EPS, op0=ALU.mult, op1=ALU.add),
                  reads=[psr[bk2[bi]]], writes=[r_rs1])
        r_m2 = R_("mu2")
        MU2 = Hf[:, 5440:5440 + T]
        kb.op("dve", lambda v: v.tensor_tensor(out=MU2, in0=MU, in1=MU, op=ALU.mult), reads=[r_mu], writes=[r_m2])
        kb.op("dve", lambda v: v.tensor_tensor(out=RS1, in0=RS1, in1=MU2, op=ALU.subtract), reads=[r_rs1, r_m2], writes=[r_rs1])
        rsqrt_ip(RS1, r_rs1)
        for i in range(16):
            q = i % 2
            kb.op("dve", lambda v, i=i: v.tensor_tensor(out=Rb[:, i, :], in0=Rb[:, i, :], in1=MU, op=ALU.subtract),
                  reads=[r_R[i], r_mu], writes=[r_R[i]])
            kb.op("dve", lambda v, i=i: v.tensor_tensor(out=Rb[:, i, :], in0=Rb[:, i, :], in1=RS1, op=ALU.mult),
                  reads=[r_R[i], r_rs1], writes=[r_R[i]])
            kb.op("act", lambda a, i=i, q=q: a.activation(out=MT[q], in_=Rb[:, i, :], func=AF.Silu,
                                                         bias=vecs[:, V_LNB + i:V_LNB + i + 1],
                                                         scale=vecs[:, V_LNG + i:V_LNG + i + 1]),
                  reads=[r_R[i], r_vecs], writes=[r_mt[q]])
            kb.op("pool" if i % 2 == 0 else "dve",
                  lambda g, i=i, q=q: g.tensor_tensor(out=SZ[:, i, :], in0=SZ[:, i, :], in1=MT[q], op=ALU.mult),
                  reads=[r_mt[q], r_SZ[i]], writes=[r_SZ[i]])
        kb.barrier()
        XSTG = [Hf[:, 6528 + i * T:6528 + (i + 1) * T] for i in range(2)]
        r_xstg = [R_("xstg2_0"), R_("xstg2_1")]
        outproj(56, x1_d, r_R, False, SZ, r_SZ, XSTG, r_xstg, r_H)
        kb.barrier()
        bks = [bank() for _ in TBS]
        for k in range(KC):
            sq = SQT[k % 2]
            rsq = R_("sqt%d" % (k % 2))
            kb.op("act", lambda a, k=k, sq=sq: a.activation(out=sq, in_=Rb[:, k, :], func=AF.Square), reads=[r_R[k]],
                  writes=[rsq])

            def mmf(pe, k=k, sq=sq):
                ins = None
                for bi, (t0, tn) in enumerate(TBS):
                    ins = pe.matmul(ps[bks[bi]][:, 0:tn], lhsT=ones[:], rhs=sq[:, t0:t0 + tn], start=(k == 0), stop=(k == KC - 1))
                return ins
            kb.op("pe", mmf, reads=[rsq, R_("ones")], writes=[psr[b] for b in bks])
        r_rstd = R_("rstd")
        for bi, (t0, tn) in enumerate(TBS):
            kb.op("dve", lambda v, bi=bi, t0=t0, tn=tn: v.tensor_scalar(out=RSTD[:, t0:t0 + tn], in0=ps[bks[bi]][:, 0:tn],
                                                                       scalar1=1.0 / D, scalar2=EPS, op0=ALU.mult, op1=ALU.add),
                  reads=[psr[bks[bi]]], writes=[r_rstd])
        rsqrt_ip(RSTD, r_rstd)
        OST = [Hf[:, i * T:(i + 1) * T] for i in range(2)]
        r_ost = [R_("ost0"), R_("ost1")]
        for k in range(KC):
            q = k % 2
            kb.op("dve", lambda v, k=k, q=q: v.scalar_tensor_tensor(out=OST[q], in0=Rb[:, k, :], scalar=vecs[:, V_FG + k:V_FG + k + 1],
                                                                   in1=RSTD, op0=ALU.mult, op1=ALU.mult),
                  reads=[r_R[k], r_rstd, r_vecs], writes=[r_ost[q]])
            kb.final.append(kb.dma("sp", yT_d[k * 128:(k + 1) * 128, 0:NP_], OST[q][:, 0:NP_], reads=[r_ost[q]],
                                   writes=[R_("yT%d" % k)], store=True))
            kb.final.append(kb.dma("sp", yT_d[k * 128:(k + 1) * 128, NP_:T], OST[q][:, NP_:T], reads=[r_ost[q]],
                                   writes=[R_("yTs%d" % k)], store=True))


    kb.op("pool", lambda g: g.memset(ust[:], 0.0), writes=[R_("ust")])
    kb.op("pool", lambda g: g.memset(a1T, 0.0), writes=[R_("a1T")])
    kb.op("pool", lambda g: g.memset(KDTMP2, 0.0), writes=[R_("kdb1")])
    wprefetch(3)
    for PS in range(NPASS):
        run_pass(PS)

    e_sp = kb.engs["sp"]
    fw = {}
    for s, v in kb.final:
        fw[s] = max(fw.get(s, 0), v)
    e_sp.prog.append((list(fw.items()), None, None, 0))

    with nc.Block() as block:
        @block.sync
        def _(en):
            kb.emit("sp", en)

        @block.gpsimd
        def _(en):
            kb.emit("pool", en)

        @block.scalar
        def _(en):
            kb.emit("act", en)

        @block.vector
        def _(en):
            kb.emit("dve", en)

        @block.tensor
        def _(en):
            kb.emit("pe", en)
    return nc


def _prep_inputs(inp):
    f = np.float32
    g = lambda k: np.asarray(inp[k], dtype=f)
    xp, xs = g("x_prompt"), g("x_sample")
    w_in0 = g("gla_w_in")[0]
    w_out0 = g("gla_w_out")[0]
    w_in1 = g("conv_w_in")[0]
    w_out1 = g("conv_w_out")[0]

    def slotify(cols):
        return np.ascontiguousarray(cols.reshape(KC, 128, 256).transpose(1, 0, 2))

    w0 = np.empty((32, 128, KC, 256), f)
    for j in range(8):
        w0[j] = slotify(np.concatenate([w_in0[:, 1024 + 128 * j:1024 + 128 * (j + 1)], w_in0[:, 128 * j:128 * (j + 1)]], 1))
    for s in range(8):
        w0[8 + s] = slotify(w_in0[:, 2048 + 256 * s:2048 + 256 * (s + 1)])
        w0[16 + s] = slotify(w_in0[:, 4096 + 256 * s:4096 + 256 * (s + 1)])
        w0[24 + s] = slotify(w_out0[:, 256 * s:256 * (s + 1)])
    w1 = np.empty((32, 128, KC, 256), f)
    for i in range(16):
        w1[i] = slotify(np.concatenate([w_in1[:, 128 * i:128 * (i + 1)], w_in1[:, 2048 + 128 * i:2048 + 128 * (i + 1)]], 1))
    for s in range(8):
        w1[16 + s] = slotify(w_in1[:, 4096 + 256 * s:4096 + 256 * (s + 1)])
        w1[24 + s] = slotify(w_out1[:, 256 * s:256 * (s + 1)])
    wa1 = np.ascontiguousarray(w_in0[:, 6144:6160].reshape(KC, 128, 16).transpose(1, 0, 2))
    wa2 = np.ascontiguousarray(g("gla_w_a2")[0])

    def pv(v):
        return v.reshape(-1, 128).T

    vecs = np.concatenate([
        pv(g("norm_g")[0]), pv(g("norm_g")[1]), pv(g("final_norm_g")), pv(g("gla_norm_g")[0]), pv(g("gla_b_a")[0]),
        pv(g("conv_b_in")[0]), pv(g("conv_b")[0]), pv(g("conv_ln_g")[0]), pv(g("conv_ln_b")[0]),
        g("conv_w")[0].reshape(31, 16, 128).transpose(2, 1, 0).reshape(128, 496),
    ], axis=1).astype(f)
    assert vecs.shape == (128, 664)
    vecs = np.ascontiguousarray(vecs)
    sgl = g("state_gla")[0]
    scv = g("state_conv")[0]
    maps = []
    zblk = np.zeros((1024, D), f)
    for c in range(8):
        b, seg = c // 4, c % 4
        blks = []
        for i in range(seg - 3, seg + 1):
            blks.append(xp[b, i * 1024:(i + 1) * 1024, :] if i >= 0 else zblk)
        xT = np.ascontiguousarray(np.concatenate(blks + [xs[c]], 0).T)
        sg = np.ascontiguousarray(sgl[c].reshape(4, 2, 128, 512).transpose(2, 0, 1, 3).reshape(128, 8, 512))
        sc = np.ascontiguousarray(scv[c].reshape(30, 16, 128).transpose(2, 1, 0))
        flag = np.full((128, 1), 1.0 if seg > 0 else 0.0, f)
        maps.append({"xT": xT, "w0": w0, "w1": w1, "wa1": wa1, "wa2": wa2, "vecs": vecs, "sg": sg, "sc": sc,
                     "flag": flag})
    return maps


_NC = None


def kernel(**inputs):
    global _NC
    maps = _prep_inputs(inputs)
    if _NC is None:
        _NC = build_nc()
    res = run_bass_kernel_spmd(_NC, maps, core_ids=list(range(8)))
    R = res.results
    f = np.float32
    y_prompt = np.empty((2, 4096, D), f)
    y_sample = np.empty((8, 64, D), f)
    gla_p = np.empty((1, 2, 4, 256, 512), f)
    conv_p = np.empty((1, 2, 30, D), f)
    gla_s = np.empty((1, 8, 4, 256, 512), f)
    conv_s = np.empty((1, 8, 30, D), f)

    def unS(a):
        return a.transpose(1, 0, 2).reshape(4, 256, 512)

    for c in range(8):
        b, seg = c // 4, c % 4
        yT = R[c]["yT"]
        y_prompt[b, seg * 1024:(seg + 1) * 1024, :] = yT[:, :1024].T
        y_sample[c] = yT[:, 1024:].T
        gla_s[0, c] = unS(R[c]["glas"])
        cst = R[c]["cst"]
        conv_s[0, c] = cst[:, :, 30:60].transpose(2, 1, 0).reshape(30, D)
        if seg == 3:
            gla_p[0, b] = unS(R[c]["glap"])
            conv_p[0, b] = cst[:, :, 0:30].transpose(2, 1, 0).reshape(30, D)
    return (y_prompt, y_sample, gla_p, conv_p, gla_s, conv_s)
```

```python
import numpy as np
import concourse.bass as bass
import concourse.mybir as mybir
from concourse.bass_utils import run_bass_kernel_spmd

F32 = mybir.dt.float32
BF16 = mybir.dt.bfloat16
AF = mybir.ActivationFunctionType
ALU = mybir.AluOpType
AX = mybir.AxisListType

D = 2048
T = 1088
NP_ = 1024
NS = 64
KC = 16
EPS = 1e-6
TBS = [(0, 512), (512, 512), (1024, 64)]
CHUNKS = [(i * 128, 128) for i in range(8)] + [(1024, 64)]
NPASS = 4
XW = NPASS * 1024 + 64
UW = 30 + 1024 + 30 + 64
USO = 30 + 1024


class Res:
    def __init__(self, name):
        self.name = name
        self.w = None
        self.r = {}
        self.lsem = None
        self.ssem = None
        self.lcnt = 0
        self.scnt = 0


class Eng:
    def __init__(self, name, sem):
        self.name = name
        self.sem = sem
        self.cnt = 0
        self.seen = {}
        self.prog = []


class Rec:
    def __init__(self):
        self.calls = []

    def __getattr__(self, name):
        def f(*a, **k):
            self.calls.append((name, a, k))
            return self
        return f


class KB:
    def __init__(self, nc):
        self.nc = nc
        self.sems = []
        self.engs = {}
        self.final = []
        self.dma_toks = {}

    def new_sem(self, name):
        s = self.nc.alloc_semaphore(name)
        self.sems.append(s)
        return len(self.sems) - 1

    def add_eng(self, name):
        self.engs[name] = Eng(name, self.new_sem("e_" + name))

    def _collect(self, e, reads, writes, is_dma):
        waits = {}

        def add(tok, raw):
            if tok is None:
                return
            s, v = tok
            if not is_dma and s == e.sem:
                if e.name == "pe" or not raw:
                    return
            if waits.get(s, 0) < v:
                waits[s] = v

        for r in reads:
            add(r.w, True)
        for w in writes:
            add(w.w, False)
            for s, v in w.r.items():
                add((s, v), False)
        out = []
        for s, v in waits.items():
            if e.seen.get(s, 0) < v:
                e.seen[s] = v
                out.append((s, v))
        return out

    def op(self, eng, fn, reads=(), writes=()):
        e = self.engs[eng]
        waits = self._collect(e, reads, writes, False)
        e.cnt += 1
        tok = (e.sem, e.cnt)
        for r in reads:
            if r.r.get(tok[0], 0) < tok[1]:
                r.r[tok[0]] = tok[1]
        for w in writes:
            w.w = tok
            w.r = {}
        rec = Rec()
        fn(rec)
        e.prog.append((waits, rec.calls, e.sem, 1))
        return tok

    def raw(self, eng, waits, calls):
        e = self.engs[eng]
        e.cnt += 1
        e.prog.append((list(waits), calls, e.sem, 1))
        return (e.sem, e.cnt)

    def dma(self, eng, out_ap, in_ap, reads=(), writes=(), store=False):
        e = self.engs[eng]
        waits = self._collect(e, reads, writes, True)
        if store:
            rs = reads[0]
            if rs.ssem is None:
                rs.ssem = self.new_sem("s_" + rs.name)
            rs.scnt += 16
            tok = (rs.ssem, rs.scnt)
        else:
            ws = writes[0]
            if ws.lsem is None:
                ws.lsem = self.new_sem("l_" + ws.name)
            ws.lcnt += 16
            tok = (ws.lsem, ws.lcnt)
        for r in reads:
            if r.r.get(tok[0], 0) < tok[1]:
                r.r[tok[0]] = tok[1]
        for w in writes:
            w.w = tok
            w.r = {}

        e.prog.append((waits, [("dma_start", (), dict(out=out_ap, in_=in_ap))], tok[0], 16))
        nm = (reads[0] if store else writes[0]).name
        if not (nm.startswith("W") and nm[1:].isdigit()):
            self.dma_toks[tok[0]] = tok[1]
        return tok

    def barrier(self, names=("pe", "act", "dve", "pool")):
        toks = [(self.engs[n].sem, self.engs[n].cnt) for n in names] + list(self.dma_toks.items())
        for n in tuple(names) + ("sp",):
            e = self.engs[n]
            waits = []
            for s, v in toks:
                if s != e.sem and v > 0 and e.seen.get(s, 0) < v:
                    e.seen[s] = v
                    waits.append((s, v))
            if waits:
                e.prog.append((waits, None, None, 0))

    def emit(self, eng, en):
        e = self.engs[eng]
        for waits, calls, sem, inc in e.prog:
            for s, v in waits:
                en.wait_ge(self.sems[s], v)
            if calls is not None:
                ins = None
                for name, a, k in calls:
                    ins = getattr(en, name)(*a, **k)
                ins.then_inc(self.sems[sem], inc)


def build_nc():
    nc = bass.Bass("TRN2", target_bir_lowering=False)
    kb = KB(nc)
    for n in ("pe", "act", "dve", "pool", "sp"):
        kb.add_eng(n)

    def din(name, shape, dt=F32):
        return nc.dram_tensor(name, shape, dt, kind="ExternalInput").ap()

    def dout(name, shape, dt=F32):
        return nc.dram_tensor(name, shape, dt, kind="ExternalOutput").ap()

    xT_d = din("xT", [D, XW])
    w0_d = din("w0", [32, 128, KC, 256])
    w1_d = din("w1", [32, 128, KC, 256])
    wa1_d = din("wa1", [128, KC, 16])
    wa2_d = din("wa2", [16, 1024])
    vecs_d = din("vecs", [128, 664])
    sg_d = din("sg", [128, 8, 512])
    sc_d = din("sc", [128, 16, 30])
    flag_d = din("flag", [128, 1])
    yT_d = dout("yT", [D, T])
    glap_d = dout("glap", [128, 8, 512])
    glas_d = dout("glas", [128, 8, 512])
    cst_d = dout("cst", [128, 16, 60])
    x1_d = nc.dram_tensor("x1stash", [D, T], F32).ap()
    sdram = nc.dram_tensor("sdram", [128, 4096], F32).ap()
    hbt_d = nc.dram_tensor("hbt", [128, KC, 128], BF16).ap()

    def sb(name, shape, dt):
        return nc.alloc_sbuf_tensor("sb_" + name, shape, dt)

    Hb = sb("Hb", [128, KC, T], BF16)
    Vb = sb("Vb", [128, 9, 2048], BF16)
    Gb = sb("Gb", [128, KC, T], BF16)
    Rb = sb("Rb", [128, KC, T], F32)
    Wb = [sb("W%d" % i, [128, KC, 256], BF16) for i in range(3)]
    vecs = sb("vecs", [128, 664], F32)
    ident = sb("ident", [128, 128], BF16)
    ones = sb("ones", [128, 128], BF16)
    trimask = sb("trimask", [128, 128], F32)
    wa1 = sb("wa1", [128, KC, 16], BF16)
    dec = sb("dec", [128, 8, 9], F32)
    nba = sb("nba", [128, 8], F32)
    ust = sb("ust", [128, 16, 60], F32)
    smalls = sb("smalls", [128, 512], F32)
    flag = sb("flag", [128, 1], F32)

    V_NG0, V_NG1, V_FG, V_GNG, V_BA, V_CBIN, V_CB, V_LNG, V_LNB, V_CW = 0, 16, 32, 48, 64, 72, 120, 136, 152, 168

    Rbf = Rb[:].rearrange("p k t -> p (k t)").bitcast(BF16)
    QE = Rbf[:, 0:8 * T].rearrange("p (j t) -> p j t", j=8)
    KE = Rbf[:, 8 * T:16 * T].rearrange("p (j t) -> p j t", j=8)
    KDT = Rbf[:, 16 * T:16 * T + 9 * 1024].rearrange("p (c d) -> p c d", c=9)
    o_tmp = 16 * T + 9 * 1024
    Rf = Rb[:].rearrange("p k t -> p (k t)")
    f_off = (o_tmp + 1) // 2
    EB = Rf[:, f_off:f_off + T]
    EMB = Rf[:, f_off + T:f_off + 2 * T]
    ED = Rf[:, f_off + 2 * T:f_off + 3 * T]
    KDTMP = Rf[:, f_off + 3 * T:f_off + 3 * T + T // 2].bitcast(BF16)
    assert f_off + 3 * T + T // 2 <= KC * T
    Vf = Vb[:].rearrange("p c d -> p (c d)").bitcast(F32)
    TMP1 = Vf[:, 0:T]
    CS = Vf[:, T:2 * T]
    RESET = Vf[:, 2 * T:3 * T]
    RSTD = Vf[:, 3 * T:4 * T]
    identf = Vf[:, 6 * T:6 * T + 128]
    wa2 = Gb[0:16, 14, 0:1024]
    a1T = Gb[0:16, 15, :]
    KDTMP2 = Vf[:, 5 * T:5 * T + T // 2].bitcast(BF16)
    SQT = [Vf[:, 4 * T + i * (T // 2):4 * T + (i + 1) * (T // 2)].bitcast(BF16) for i in range(2)]
    Hf = Hb[:].rearrange("p k t -> p (k t)").bitcast(F32)
    S32 = Hf[:, 0:4096].rearrange("p (j v) -> p j v", j=8)
    SBF = Hf[:, 4096:6144].bitcast(BF16).rearrange("p (j v) -> p j v", j=8)
    AM = [Hf[:, 6144 + i * 64:6144 + (i + 1) * 64].bitcast(BF16) for i in range(2)]
    SQO = [Hf[:, 6272 + i * 256:6272 + (i + 1) * 256].bitcast(BF16) for i in range(2)]
    RSO = [Hf[:, 6784 + i * 128:6784 + (i + 1) * 128] for i in range(2)]
    OT = [Hf[:, 7040 + i * 512:7040 + (i + 1) * 512] for i in range(2)]
    STG = Hf[:, 0:0]
    XST = [Hf[:, 8064 - 0 + 0:8064] for _ in range(0)]

    ps = [nc.alloc_psum_tensor("ps%d" % i, [128, 512], F32) for i in range(8)]
    psr = [Res("ps%d" % i) for i in range(8)]
    bank_ctr = [0]

    def bank():
        b = bank_ctr[0] % 8
        bank_ctr[0] += 1
        return b

    r_vecs = Res("consts")
    r_H = [Res("H%d" % k) for k in range(KC)]
    r_V = [Res("V%d" % c) for c in range(9)]
    r_G = [Res("G%d" % k) for k in range(KC)]
    r_R = [Res("R%d" % k) for k in range(KC)]
    r_W = [Res("W%d" % i) for i in range(3)]
    r_misc = {}

    rl_cache = {}

    def RL(name, n):
        if name not in rl_cache:
            rl_cache[name] = [Res("%s%d" % (name, i)) for i in range(n)]
        return rl_cache[name]

    def R_(name):
        if name not in r_misc:
            r_misc[name] = Res(name)
        return r_misc[name]

    kb.dma("sp", vecs[:], vecs_d[:, :], writes=[r_vecs])
    kb.dma("sp", flag[:], flag_d[:, :], writes=[r_vecs])
    kb.dma("pool", wa1[:], wa1_d[:, :, :], writes=[R_("wa1")])
    r_c = R_("cgen")
    kb.op("pool", lambda g: g.memset(identf, 0.0), writes=[r_c])
    kb.op("pool", lambda g: g.affine_select(out=identf, in_=identf, compare_op=ALU.not_equal, fill=1.0,
                                            base=0, pattern=[[-1, 128]], channel_multiplier=1),
          reads=[r_c], writes=[r_c])
    kb.op("pool", lambda g: g.tensor_copy(out=ident[:], in_=identf), reads=[r_c], writes=[R_("ident")])
    kb.op("pool", lambda g: g.memset(ones[:], 1.0), writes=[R_("ones")])
    r_tm = R_("trimask")
    kb.op("pool", lambda g: g.memset(trimask[:], 1.0), writes=[r_tm])
    kb.op("pool", lambda g: g.affine_select(out=trimask[:], in_=trimask[:], compare_op=ALU.is_ge, fill=0.0,
                                            base=0, pattern=[[1, 128]], channel_multiplier=-1),
          reads=[r_tm], writes=[r_tm])
    kb.op("dve", lambda v: v.tensor_scalar(out=nba[:], in0=vecs[:, V_BA:V_BA + 8], scalar1=-1.0, scalar2=None,
                                           op0=ALU.mult), reads=[r_vecs], writes=[R_("nba")])

    wq = {"n": 0}

    def wload(src_ap):
        i = wq["n"] % 3
        wq["n"] += 1
        kb.dma("pool", Wb[i][:], src_ap, writes=[r_W[i]])
        return i

    MODES = ["state", "state", "tou", "main"]
    NSLOT = {"state": 16, "tou": 32, "main": 64}
    wsched = []
    wmap = {}
    for _p in range(NPASS):
        for _l in range(NSLOT[MODES[_p]]):
            wmap[(_p, _l)] = len(wsched)
            wsched.append(w0_d[_l] if _l < 32 else w1_d[_l - 32])
    wstate = {"issued": 0, "slots": []}

    def wprefetch(upto):
        while wstate["issued"] < min(upto, len(wsched)):
            wstate["slots"].append(wload(wsched[wstate["issued"]]))
            wstate["issued"] += 1

    def wslot(idx):
        wprefetch(idx + 3)
        return wstate["slots"][idx]

    def rsqrt_ip(ap, res):
        kb.op("act", lambda a: a.activation(out=ap, in_=ap, func=AF.Ln), reads=[res], writes=[res])
        kb.op("act", lambda a: a.activation(out=ap, in_=ap, func=AF.Exp, scale=-0.5), reads=[res], writes=[res])

    def rms_prep(xsrc, xres, gcol, tbs=None):
        tbs = tbs or TBS
        lo = min(t0 for t0, tn in tbs)
        hi = max(t0 + tn for t0, tn in tbs)
        bks = [bank() for _ in tbs]
        for k in range(KC):
            sq = SQT[k % 2]
            rsq = R_("sqt%d" % (k % 2))
            kb.op("act", lambda a, k=k, sq=sq: a.activation(out=sq[:, lo:hi], in_=xsrc[:, k, lo:hi], func=AF.Square),
                  reads=[xres[k]], writes=[rsq])

            def mm(pe, k=k, sq=sq):
                ins = None
                for bi, (t0, tn) in enumerate(tbs):
                    ins = pe.matmul(ps[bks[bi]][:, 0:tn], lhsT=ones[:], rhs=sq[:, t0:t0 + tn], start=(k == 0),
                                    stop=(k == KC - 1))
                return ins
            kb.op("pe", mm, reads=[rsq, R_("ones")], writes=[psr[b] for b in bks])
        r_rstd = R_("rstd")
        for bi, (t0, tn) in enumerate(tbs):
            kb.op("dve", lambda v, bi=bi, t0=t0, tn=tn: v.tensor_scalar(
                out=RSTD[:, t0:t0 + tn], in0=ps[bks[bi]][:, 0:tn], scalar1=1.0 / D, scalar2=EPS, op0=ALU.mult,
                op1=ALU.add), reads=[psr[bks[bi]]], writes=[r_rstd])
        rsqrt_ip(RSTD[:, lo:hi], r_rstd)
        for k in range(KC):
            kb.op("dve", lambda v, k=k: v.scalar_tensor_tensor(
                out=Hb[:, k, lo:hi], in0=xsrc[:, k, lo:hi], scalar=vecs[:, gcol + k:gcol + k + 1], in1=RSTD[:, lo:hi],
                op0=ALU.mult, op1=ALU.mult), reads=[xres[k], r_rstd, r_vecs], writes=[r_H[k]])

    cur = {"tbs": TBS}
    TB_TAIL = [(NP_ - 128, 128)]

    def proj_ws(slot, col0, evac, tbs=None):
        tbs = tbs or TBS
        cur["tbs"] = tbs
        bks = [bank() for _ in tbs]

        def mm(pe):
            ins = None
            for k in range(KC):
                for bi, (t0, tn) in enumerate(tbs):
                    ins = pe.matmul(ps[bks[bi]][:, 0:tn], lhsT=Wb[slot][:, k, col0:col0 + 128],
                                    rhs=Hb[:, k, t0:t0 + tn], start=(k == 0), stop=(k == KC - 1))
            return ins
        kb.op("pe", mm, reads=[r_W[slot]] + r_H, writes=[psr[b] for b in bks])
        evac(bks)
        cur["tbs"] = TBS

    def run_pass(PS):
        mode = MODES[PS]
        TBX = TB_TAIL if mode == "tou" else TBS
        TBA = TBS if mode == "main" else TBS[0:2]
        kb.dma("pool", wa2, wa2_d[:, :], writes=[R_("wa2"), r_G[14]])
        for k in range(KC):
            kb.dma("sp", Rb[:, k, 0:NP_], xT_d[k * 128:(k + 1) * 128, PS * NP_:(PS + 1) * NP_], writes=[r_R[k]])
            if mode == "main":
                kb.dma("sp", Rb[:, k, NP_:T], xT_d[k * 128:(k + 1) * 128, NPASS * NP_:XW], writes=[r_R[k]])
        r_reset = R_("reset")
        kb.op("pool", lambda g: g.memset(RESET, 1.0), writes=[r_reset])
        kb.op("pool", lambda g: g.memset(RESET.rearrange("p (c t) -> p c t", t=64)[:, 0:17:2, 0:1], 0.0),
              reads=[r_reset], writes=[r_reset])
        rms_prep(Rb, r_R, V_NG0, TBA)
        kb.barrier()

        bks = [bank() for _ in TBA]

        def mm_a1(pe):
            ins = None
            for k in range(KC):
                for bi, (t0, tn) in enumerate(TBA):
                    ins = pe.matmul(ps[bks[bi]][0:16, 0:tn], lhsT=wa1[:, k, :], rhs=Hb[:, k, t0:t0 + tn],
                                    start=(k == 0), stop=(k == KC - 1))
            return ins
        kb.op("pe", mm_a1, reads=[R_("wa1")] + r_H, writes=[psr[b] for b in bks])
        r_a1 = R_("a1T")
        for bi, (t0, tn) in enumerate(TBA):
            kb.op("act", lambda a, bi=bi, t0=t0, tn=tn: a.activation(out=a1T[:, t0:t0 + tn], in_=ps[bks[bi]][0:16, 0:tn],
                                                                    func=AF.Copy),
                  reads=[psr[bks[bi]]], writes=[r_a1])

        r_tmp1, r_cs, r_eb, r_emb, r_ed, r_kdtmp = (R_(n) for n in ("tmp1", "cs", "eb", "emb", "ed", "kdtmp"))
        r_dec = R_("dec")
        r_QE = RL("QE", 8)
        r_KE = RL("KE", 8)
        r_KDT = RL("KDT", 8)
        KDB = [KDTMP, KDTMP2]
        r_kdb = [R_("kdb0"), R_("kdb1")]
        pending_tr = [None]
        for j in range(8):
            bks = [bank() for _ in TBS]

            def mm_z(pe, j=j, bks=bks):
                ins = None
                for bi, (t0, tn) in enumerate(TBS):
                    ins = pe.matmul(ps[bks[bi]][:, 0:tn], lhsT=wa2[:, j * 128:(j + 1) * 128], rhs=a1T[:, t0:t0 + tn],
                                    start=True, stop=True)
                return ins
            kb.op("pe", mm_z, reads=[R_("wa2"), r_a1], writes=[psr[b] for b in bks])
            for bi, (t0, tn) in enumerate(TBS):
                kb.op("act", lambda a, j=j, bi=bi, t0=t0, tn=tn, bks=bks: a.activation(
                    out=TMP1[:, t0:t0 + tn], in_=ps[bks[bi]][:, 0:tn], func=AF.Exp, bias=nba[:, j:j + 1], scale=-1.0),
                    reads=[psr[bks[bi]], R_("nba")], writes=[r_tmp1])
            kb.op("act", lambda a: a.activation(out=TMP1, in_=TMP1, func=AF.Ln, bias=1.0, scale=1.0),
                  reads=[r_tmp1], writes=[r_tmp1])
            kb.op("dve", lambda v: v.tensor_tensor_scan(out=CS, data0=RESET, data1=TMP1, initial=0.0, op0=ALU.mult,
                                                        op1=ALU.add), reads=[r_tmp1, r_reset], writes=[r_cs])
            if mode != "state":
                kb.op("act", lambda a: a.activation(out=EB, in_=CS, func=AF.Exp, scale=-1.0 / 16), reads=[r_cs], writes=[r_eb])
                kb.op("act", lambda a: a.activation(out=EMB, in_=CS, func=AF.Exp, scale=1.0 / 16), reads=[r_cs], writes=[r_emb])
            kb.op("act", lambda a, j=j: a.activation(out=dec[:, j, 0:8], in_=CS[:, 127:1024:128], func=AF.Exp,
                                                     scale=-1.0 / 16), reads=[r_cs], writes=[r_dec])
            kb.op("act", lambda a, j=j: a.activation(out=dec[:, j, 8:9], in_=CS[:, 1087:1088], func=AF.Exp,
                                                     scale=-1.0 / 16), reads=[r_cs], writes=[r_dec])
            kb.op("dve", lambda v: v.tensor_tensor(
                out=ED[:, 0:1024].rearrange("p (c t) -> p c t", c=8), in0=CS[:, 0:1024].rearrange("p (c t) -> p c t", c=8),
                in1=CS[:, 127:1024:128].unsqueeze(2).to_broadcast([128, 8, 128]), op=ALU.subtract),
                reads=[r_cs], writes=[r_ed])
            kb.op("dve", lambda v: v.tensor_scalar(out=ED[:, 1024:1088], in0=CS[:, 1024:1088], scalar1=CS[:, 1087:1088],
                                                   scalar2=None, op0=ALU.subtract), reads=[r_cs], writes=[r_ed])
            kb.op("act", lambda a: a.activation(out=ED, in_=ED, func=AF.Exp, scale=1.0 / 16), reads=[r_ed], writes=[r_ed])
            slot = wslot(wmap[(PS, j)])

            def evac_k(bks, j=j):
                for bi, (t0, tn) in enumerate(cur["tbs"]):
                    if mode != "state":
                        kb.op("dve", lambda v, bi=bi, t0=t0, tn=tn: v.tensor_tensor(
                            out=KE[:, j, t0:t0 + tn], in0=ps[bks[bi]][:, 0:tn], in1=EMB[:, t0:t0 + tn], op=ALU.mult),
                            reads=[psr[bks[bi]], r_emb], writes=[r_KE[j]])
                    kb.op("dve", lambda v, bi=bi, t0=t0, tn=tn: v.tensor_tensor(
                        out=KDB[j % 2][:, t0:t0 + tn], in0=ps[bks[bi]][:, 0:tn], in1=ED[:, t0:t0 + tn], op=ALU.mult),
                        reads=[psr[bks[bi]], r_ed], writes=[r_kdb[j % 2]])
            proj_ws(slot, 0, evac_k, TBA)
            def transposes(j=j):
                nchk = 9 if mode == "main" else 8
                for c0 in range(0, nchk, 4):
                    cs_ = list(range(c0, min(c0 + 4, nchk)))
                    b = bank()
                    pbf = ps[b][:].bitcast(BF16)

                    def tr(pe, cs_=cs_, pbf=pbf):
                        ins = None
                        for ii, c in enumerate(cs_):
                            t0, tn = CHUNKS[c]
                            ins = pe.transpose(out=pbf[0:tn, ii * 128:(ii + 1) * 128], in_=KDB[j % 2][:, t0:t0 + tn],
                                               identity=ident[:])
                        return ins
                    kb.op("pe", tr, reads=[r_kdb[j % 2], R_("ident")], writes=[psr[b]])
                    for ii, c in enumerate(cs_):
                        tn = CHUNKS[c][1]
                        kb.op("act", lambda a, ii=ii, c=c, tn=tn, pbf=pbf, b=b: a.activation(
                            out=KDT[0:tn, c, j * 128:(j + 1) * 128], in_=pbf[0:tn, ii * 128:(ii + 1) * 128], func=AF.Copy),
                            reads=[psr[b]], writes=[r_KDT[j]])

            def evac_q(bks, j=j):
                for bi, (t0, tn) in enumerate(cur["tbs"]):
                    kb.op("dve", lambda v, bi=bi, t0=t0, tn=tn: v.scalar_tensor_tensor(
                        out=QE[:, j, t0:t0 + tn], in0=ps[bks[bi]][:, 0:tn], scalar=0.0625, in1=EB[:, t0:t0 + tn],
                        op0=ALU.mult, op1=ALU.mult), reads=[psr[bks[bi]], r_eb], writes=[r_QE[j]])
            if mode != "state":
                proj_ws(slot, 128, evac_q, TBX)
            if pending_tr[0] is not None:
                pending_tr[0]()
            pending_tr[0] = transposes
        pending_tr[0]()

        kb.barrier()
        for s in range(8):
            slot = wslot(wmap[(PS, 8 + s)])
            for c, (t0, tn) in enumerate(CHUNKS):
                if mode != "main" and c == 8:
                    continue
                b = bank()

                def mm_v(pe, t0=t0, tn=tn, b=b, slot=slot):
                    ins = None
                    for k in range(KC):
                        ins = pe.matmul(ps[b][0:tn, 0:256], lhsT=Hb[:, k, t0:t0 + tn], rhs=Wb[slot][:, k, :],
                                        start=(k == 0), stop=(k == KC - 1))
                    return ins
                kb.op("pe", mm_v, reads=[r_W[slot]] + r_H, writes=[psr[b]])
                eng = "act" if c % 2 == 0 else "dve"
                if eng == "act":
                    kb.op("act", lambda a, c=c, tn=tn, b=b, s=s: a.activation(
                        out=Vb[0:tn, c, s * 256:(s + 1) * 256], in_=ps[b][0:tn, 0:256], func=AF.Copy),
                        reads=[psr[b]], writes=[r_V[c]])
                else:
                    kb.op("dve", lambda v, c=c, tn=tn, b=b, s=s: v.tensor_copy(
                        out=Vb[0:tn, c, s * 256:(s + 1) * 256], in_=ps[b][0:tn, 0:256]),
                        reads=[psr[b]], writes=[r_V[c]])
        for s in range(8 if mode != "state" else 0):
            slot = wslot(wmap[(PS, 16 + s)])
            for half in range(2):
                oc = 2 * s + half

                def evac_r(bks, oc=oc):
                    for bi, (t0, tn) in enumerate(cur["tbs"]):
                        kb.op("act", lambda a, bi=bi, t0=t0, tn=tn: a.activation(
                            out=Gb[:, oc, t0:t0 + tn], in_=ps[bks[bi]][:, 0:tn], func=AF.Silu),
                            reads=[psr[bks[bi]]], writes=[r_G[oc]])
                    glo = min(t0 for t0, tn in cur["tbs"])
                    ghi = max(t0 + tn for t0, tn in cur["tbs"])
                    kb.op("dve", lambda g: g.tensor_scalar(out=Gb[:, oc, glo:ghi], in0=Gb[:, oc, glo:ghi],
                                                            scalar1=vecs[:, V_GNG + oc:V_GNG + oc + 1], scalar2=None,
                                                            op0=ALU.mult), reads=[r_G[oc], r_vecs], writes=[r_G[oc]])
                proj_ws(slot, half * 128, evac_r, TBX)
        kb.barrier()

        r_S = RL("S", 8)
        r_SB = RL("SB", 8)
        r_sd = R_("sdram")
        if PS == 0:
            kb.op("pool", lambda g: g.memset(Hf[:, 0:4096], 0.0), writes=r_S)
        else:
            kb.dma("sp", Hf[:, 0:4096], sdram[:, :], reads=[r_sd], writes=r_S + r_H)

        def state_update(c, j, tn, with_bf):
            h = j // 2
            b = bank()
            kb.op("pe", lambda pe: pe.matmul(ps[b][:, :], lhsT=KDT[0:tn, c, j * 128:(j + 1) * 128],
                                             rhs=Vb[0:tn, c, h * 512:(h + 1) * 512], start=True, stop=True),
                  reads=[r_KDT[j], r_V[c]], writes=[psr[b]])
            kb.op("dve", lambda v: v.scalar_tensor_tensor(out=S32[:, j, :], in0=S32[:, j, :], scalar=dec[:, j, c:c + 1],
                                                          in1=ps[b][:, :], op0=ALU.mult, op1=ALU.add),
                  reads=[psr[b], r_S[j], r_dec], writes=[r_S[j]])
            if with_bf:
                kb.op("act", lambda a: a.activation(out=SBF[:, j, :], in_=S32[:, j, :], func=AF.Copy),
                      reads=[r_S[j]], writes=[r_SB[j]])

        for j in range(8 if mode == "main" else 0):
            kb.op("act", lambda a, j=j: a.activation(out=SBF[:, j, :], in_=S32[:, j, :], func=AF.Copy), reads=[r_S[j]],
                  writes=[r_SB[j]])

        r_am = [R_("am0"), R_("am1")]
        r_sqo = [R_("sqo0"), R_("sqo1")]
        r_rso = [R_("rso0"), R_("rso1")]
        r_ot = [R_("ot0"), R_("ot1")]
        it = [0]

        def chunk_head(c, h):
            t0, tn = CHUNKS[c]
            q = it[0] % 2
            it[0] += 1
            bA = bank()

            def mmA(pe):
                ins = None
                for dd in range(2):
                    j = 2 * h + dd
                    ins = pe.matmul(ps[bA][0:tn, 0:tn], lhsT=KE[:, j, t0:t0 + tn], rhs=QE[:, j, t0:t0 + tn],
                                    start=(dd == 0), stop=(dd == 1))
                return ins
            kb.op("pe", mmA, reads=[r_KE[2 * h], r_KE[2 * h + 1], r_QE[2 * h], r_QE[2 * h + 1]], writes=[psr[bA]])
            kb.op("dve", lambda v: v.tensor_tensor(out=AM[q][0:tn, 0:tn], in0=ps[bA][0:tn, 0:tn], in1=trimask[0:tn, 0:tn],
                                                   op=ALU.mult), reads=[psr[bA], r_tm], writes=[r_am[q]])
            bO = bank()
            po = ps[bO][:].rearrange("p (m t) -> p m t", m=4)

            def mmO(pe):
                ins = None
                for m in range(4):
                    ins = pe.matmul(po[:, m, 0:tn], lhsT=Vb[0:tn, c, h * 512 + m * 128:h * 512 + (m + 1) * 128],
                                    rhs=AM[q][0:tn, 0:tn], start=True, stop=False)
                    for dd in range(2):
                        j = 2 * h + dd
                        ins = pe.matmul(po[:, m, 0:tn], lhsT=SBF[:, j, m * 128:(m + 1) * 128], rhs=QE[:, j, t0:t0 + tn],
                                        start=False, stop=(dd == 1))
                return ins
            kb.op("pe", mmO, reads=[r_V[c], r_am[q], r_SB[2 * h], r_SB[2 * h + 1], r_QE[2 * h], r_QE[2 * h + 1]],
                  writes=[psr[bO]])
            sqv = SQO[q].rearrange("p (m t) -> p m t", m=4)
            kb.op("act", lambda a: a.activation(out=sqv[:, :, 0:tn], in_=po[:, :, 0:tn], func=AF.Square),
                  reads=[psr[bO]], writes=[r_sqo[q]])
            bS = bank()

            def mmS(pe):
                ins = None
                for m in range(4):
                    ins = pe.matmul(ps[bS][:, 0:tn], lhsT=ones[:], rhs=sqv[:, m, 0:tn], start=(m == 0), stop=(m == 3))
                return ins
            kb.op("pe", mmS, reads=[r_sqo[q], R_("ones")], writes=[psr[bS]])
            kb.op("dve", lambda v: v.tensor_scalar(out=RSO[q][:, 0:tn], in0=ps[bS][:, 0:tn], scalar1=1.0 / 512, scalar2=EPS,
                                                   op0=ALU.mult, op1=ALU.add), reads=[psr[bS]], writes=[r_rso[q]])
            rsqrt_ip(RSO[q][:, 0:tn], r_rso[q])
            otv = OT[q].rearrange("p (m t) -> p m t", m=4)
            kb.op("dve", lambda v: v.tensor_tensor(out=otv[:, :, 0:tn], in0=po[:, :, 0:tn],
                                                   in1=RSO[q][:, 0:tn].unsqueeze(1).to_broadcast([128, 4, tn]), op=ALU.mult),
                  reads=[psr[bO], r_rso[q]], writes=[r_ot[q]])
            gr = [r_G[4 * h + m] for m in range(4)]
            kb.op("pool", lambda g: g.tensor_tensor(out=Gb[:, 4 * h:4 * h + 4, t0:t0 + tn], in0=Gb[:, 4 * h:4 * h + 4, t0:t0 + tn],
                                                    in1=otv[:, :, 0:tn], op=ALU.mult), reads=[r_ot[q]] + gr, writes=gr)
            for dd in range(2):
                state_update(c, 2 * h + dd, tn, True)

        for c in range(8):
            if mode == "state" or (mode == "tou" and c < 7):
                for j in range(8):
                    state_update(c, j, 128, False)
            else:
                if mode == "tou":
                    for j in range(8):
                        kb.op("act", lambda a, j=j: a.activation(out=SBF[:, j, :], in_=S32[:, j, :], func=AF.Copy),
                              reads=[r_S[j]], writes=[r_SB[j]])
                for h in range(4):
                    chunk_head(c, h)
        if PS == NPASS - 1:
            kb.final.append(kb.dma("sp", glap_d.rearrange("p j v -> p (j v)"), Hf[:, 0:4096], reads=r_S + r_H, writes=[R_("glap")],
                                   store=True))
        else:
            kb.dma("sp", sdram[:, :], Hf[:, 0:4096], reads=r_S + r_H, writes=[r_sd], store=True)
        if mode == "state":
            kb.barrier()
            return
        if mode == "main":
            kb.dma("sp", Hf[:, 0:4096], sg_d.rearrange("p j v -> p (j v)"), writes=r_S)
            for j in range(8):
                kb.op("act", lambda a, j=j: a.activation(out=SBF[:, j, :], in_=S32[:, j, :], func=AF.Copy), reads=[r_S[j]],
                      writes=[r_SB[j]])
            for h in range(4):
                chunk_head(8, h)
            kb.final.append(kb.dma("sp", glas_d.rearrange("p j v -> p (j v)"), Hf[:, 0:4096], reads=r_S + r_H, writes=[R_("glas")],
                                   store=True))
        kb.barrier()

        XSTG = [Vf[:, i * T:(i + 1) * T] for i in range(2)]
        r_xstg = [R_("xstg0"), R_("xstg1")]
        r_x1d = RL("x1d", KC)

        def outproj(wbase, xsrc_d, dst_res, stash, Gsrc, r_Gsrc, XSTG, r_xstg, alias, tbs=None, do_store=True):
            tbs = tbs or TBS
            for s in range(8):
                slot = wslot(wmap[(PS, wbase + s)])
                for half in range(2):
                    oc = 2 * s + half
                    q = oc % 2
                    if stash:
                        kb.dma("sp", XSTG[q][:, 0:NP_], xsrc_d[oc * 128:(oc + 1) * 128, PS * NP_:(PS + 1) * NP_],
                               writes=[r_xstg[q]] + alias)
                        kb.dma("sp", XSTG[q][:, NP_:T], xsrc_d[oc * 128:(oc + 1) * 128, NPASS * NP_:XW], writes=[r_xstg[q]])
                    else:
                        kb.dma("sp", XSTG[q], xsrc_d[oc * 128:(oc + 1) * 128, :], reads=[r_x1d[oc]],
                               writes=[r_xstg[q]] + alias)
                    bks = [bank() for _ in tbs]

                    def mm(pe, slot=slot, half=half, bks=bks):
                        ins = None
                        for k in range(KC):
                            for bi, (t0, tn) in enumerate(tbs):
                                ins = pe.matmul(ps[bks[bi]][:, 0:tn], lhsT=Wb[slot][:, k, half * 128:(half + 1) * 128],
                                                rhs=Gsrc[:, k, t0:t0 + tn], start=(k == 0), stop=(k == KC - 1))
                        return ins
                    kb.op("pe", mm, reads=[r_W[slot]] + r_Gsrc, writes=[psr[b] for b in bks])
                    for bi, (t0, tn) in enumerate(tbs):
                        kb.op("dve", lambda v, bi=bi, t0=t0, tn=tn, bks=bks, oc=oc, q=q: v.tensor_tensor(
                            out=Rb[:, oc, t0:t0 + tn], in0=ps[bks[bi]][:, 0:tn], in1=XSTG[q][:, t0:t0 + tn], op=ALU.add),
                            reads=[psr[bks[bi]], r_xstg[q]], writes=[dst_res[oc]])
                    if stash and do_store:
                        kb.dma("sp", x1_d[oc * 128:(oc + 1) * 128, :], Rb[:, oc, :], reads=[dst_res[oc]], writes=[r_x1d[oc]],
                               store=True)

        outproj(24, xT_d, r_R, True, Gb, r_G, XSTG, r_xstg, r_V, TBX, mode != "tou")
        kb.barrier()

        rms_prep(Rb, r_R, V_NG1, TBX)
        if mode == "tou":
            kb.dma("sp", hbt_d[:, :, :], Hb[:, :, NP_ - 128:NP_], reads=r_H, writes=[R_("hbt")], store=True)
            kb.barrier()
            return
        kb.barrier()
        U = Vb[:].rearrange("p c d -> p (c d)")[:, 0:16 * UW].rearrange("p (k t) -> p k t", k=16)
        r_U = RL("U", 16)
        r_ust = R_("ust")
        SZ = Gb
        r_SZ = r_G
        Rf2 = Rb[:].rearrange("p k t -> p (k t)")
        Gbf = Gb[:].rearrange("p k t -> p (k t)")
        HBT = Gbf[:, 0:KC * 128].rearrange("p (k t) -> p k t", k=KC)
        Gf2 = Gbf.bitcast(F32)
        ATl = Gf2[:, 1024:1152]
        SGTl = Gf2[:, 1152:1280]
        r_hbt, r_atl, r_sgtl = R_("HBT"), R_("ATl"), R_("SGTl")
        kb.dma("sp", HBT, hbt_d[:, :, :], reads=[R_("hbt")], writes=[r_hbt] + r_G)
        A32 = [Rf2[:, (2 * i) * T:(2 * i + 1) * T] for i in range(2)]
        SG32 = [Rf2[:, (2 * i + 1) * T:(2 * i + 2) * T] for i in range(2)]
        r_a32 = [R_("a32_0"), R_("a32_1")]
        r_sg32 = [R_("sg32_0"), R_("sg32_1")]
        for i in range(16):
            slot = wslot(wmap[(PS, 32 + i)])
            q = i % 2

            def evac_a(bks, i=i, q=q):
                for bi, (t0, tn) in enumerate(cur["tbs"]):
                    kb.op("act", lambda a, bi=bi, t0=t0, tn=tn: a.activation(
                        out=A32[q][:, t0:t0 + tn], in_=ps[bks[bi]][:, 0:tn], func=AF.Identity,
                        bias=vecs[:, V_CBIN + i:V_CBIN + i + 1], scale=1.0), reads=[psr[bks[bi]], r_vecs], writes=[r_a32[q]])
            proj_ws(slot, 0, evac_a, TBX)

            def evac_g(bks, i=i, q=q):
                for bi, (t0, tn) in enumerate(cur["tbs"]):
                    kb.op("act", lambda a, bi=bi, t0=t0, tn=tn: a.activation(
                        out=SG32[q][:, t0:t0 + tn], in_=ps[bks[bi]][:, 0:tn], func=AF.Sigmoid,
                        bias=vecs[:, V_CBIN + 16 + i:V_CBIN + 16 + i + 1], scale=1.0),
                        reads=[psr[bks[bi]], r_vecs], writes=[r_sg32[q]])
            proj_ws(slot, 128, evac_g, TBX)
            for half, dst, rdst, fn, bcol in ((0, ATl, r_atl, AF.Identity, V_CBIN + i), (1, SGTl, r_sgtl, AF.Sigmoid, V_CBIN + 16 + i)):
                bt = bank()

                def mmt(pe, slot=slot, half=half, bt=bt):
                    ins = None
                    for k in range(KC):
                        ins = pe.matmul(ps[bt][:, 0:128], lhsT=Wb[slot][:, k, half * 128:(half + 1) * 128], rhs=HBT[:, k, :],
                                        start=(k == 0), stop=(k == KC - 1))
                    return ins
                kb.op("pe", mmt, reads=[r_W[slot], r_hbt, r_G[0], r_G[1]], writes=[psr[bt]])
                kb.op("act", lambda a, bt=bt, dst=dst, fn=fn, bcol=bcol: a.activation(
                    out=dst, in_=ps[bt][:, 0:128], func=fn, bias=vecs[:, bcol:bcol + 1], scale=1.0),
                    reads=[psr[bt], r_vecs], writes=[rdst])
            kb.op("dve", lambda v, i=i: v.scalar_tensor_tensor(out=U[:, i, 0:30], in0=ATl[:, 98:128], scalar=flag[:, 0:1],
                                                              in1=SGTl[:, 98:128], op0=ALU.mult, op1=ALU.mult),
                  reads=[r_atl, r_sgtl, r_vecs, r_G[1], r_G[2]], writes=[r_U[i]])
            if mode == "main":
                kb.op("dve", lambda v, i=i, q=q: v.tensor_tensor(out=U[:, i, 30:30 + NP_], in0=A32[q][:, 0:NP_],
                                                                in1=SG32[q][:, 0:NP_], op=ALU.mult),
                      reads=[r_a32[q], r_sg32[q]], writes=[r_U[i]])
                kb.op("dve", lambda v, i=i, q=q: v.tensor_tensor(out=U[:, i, USO + 30:USO + 30 + NS], in0=A32[q][:, NP_:T],
                                                                in1=SG32[q][:, NP_:T], op=ALU.mult),
                      reads=[r_a32[q], r_sg32[q]], writes=[r_U[i]])
            kb.op("dve", lambda v, i=i, q=q: v.tensor_tensor(out=ust[:, i, 0:30], in0=A32[q][:, NP_ - 30:NP_],
                                                            in1=SG32[q][:, NP_ - 30:NP_], op=ALU.mult),
                  reads=[r_a32[q], r_sg32[q]], writes=[r_ust])
            if mode == "main":
                kb.op("dve", lambda v, i=i, q=q: v.tensor_tensor(out=ust[:, i, 30:60], in0=A32[q][:, T - 30:T],
                                                                in1=SG32[q][:, T - 30:T], op=ALU.mult),
                      reads=[r_a32[q], r_sg32[q]], writes=[r_ust])
        if mode == "tou":
            kb.barrier()
            return
        kb.final.append(kb.dma("sp", cst_d.rearrange("p k r -> p (k r)"), ust[:].rearrange("p k r -> p (k r)"),
                               reads=[r_ust], writes=[R_("cst")], store=True))
        for s in range(8):
            slot = wslot(wmap[(PS, 48 + s)])
            for half in range(2):
                oc = 2 * s + half

                def evac_z(bks, oc=oc):
                    for bi, (t0, tn) in enumerate(cur["tbs"]):
                        kb.op("act", lambda a, bi=bi, t0=t0, tn=tn: a.activation(
                            out=SZ[:, oc, t0:t0 + tn], in_=ps[bks[bi]][:, 0:tn], func=AF.Silu,
                            bias=vecs[:, V_CBIN + 32 + oc:V_CBIN + 32 + oc + 1], scale=1.0),
                            reads=[psr[bks[bi]], r_vecs], writes=[r_SZ[oc]])
                proj_ws(slot, half * 128, evac_z)
        kb.barrier()
        SCS = Rf2[:, 9 * T:9 * T + 480]
        r_scs = R_("scs")
        kb.dma("sp", SCS, sc_d.rearrange("p k r -> p (k r)"), writes=[r_scs, r_R[9]])
        kb.op("dve", lambda v: v.tensor_copy(out=U[:, :, USO:USO + 30], in_=SCS.rearrange("p (k r) -> p k r", k=16)),
              reads=[r_scs], writes=r_U)
        kb.barrier()
        Hbf = Hb[:].rearrange("p k t -> p (k t)")
        DG = [Hbf[:, q * 31 * 128:(q + 1) * 31 * 128].rearrange("p (j m) -> p j m", j=31) for q in range(2)]
        r_dg = [RL("dg0_", 31), RL("dg1_", 31)]
        CW = vecs[:, V_CW:V_CW + 496].rearrange("p (i j) -> p i j", i=16)
        CONVB = [(0, 512, 0), (512, 512, 512), (USO, 64, 1024)]
        def dg_build(i):
            q = i % 2
            for j in range(31):
                if j % 2 == 0:
                    kb.op("act", lambda a, i=i, j=j, q=q: a.activation(out=DG[q][:, j, :], in_=ident[:], func=AF.Copy,
                                                                      scale=CW[:, i, j:j + 1]),
                          reads=[R_("ident"), r_vecs], writes=[r_dg[q][j]])
                else:
                    kb.op("dve", lambda g, i=i, j=j, q=q: g.tensor_scalar(out=DG[q][:, j, :], in0=ident[:],
                                                                         scalar1=CW[:, i, j:j + 1], scalar2=None, op0=ALU.mult),
                          reads=[R_("ident"), r_vecs], writes=[r_dg[q][j]])

        dg_build(0)
        for i in range(16):
            q = i % 2
            if i + 1 < 16:
                dg_build(i + 1)
            bks = [bank() for _ in CONVB]

            def mmc(pe, i=i, q=q, bks=bks):
                ins = None
                for j in range(31):
                    for bi, (u0, n_, o0) in enumerate(CONVB):
                        ins = pe.matmul(ps[bks[bi]][:, 0:n_], lhsT=DG[q][:, j, :], rhs=U[:, i, u0 + j:u0 + j + n_],
                                        start=(j == 0), stop=(j == 30))
                return ins
            kb.op("pe", mmc, reads=r_dg[q] + [r_U[i]], writes=[psr[b] for b in bks])
            for bi, (u0, n_, o0) in enumerate(CONVB):
                kb.op("act", lambda a, i=i, bi=bi, n_=n_, o0=o0, bks=bks: a.activation(
                    out=Rb[:, i, o0:o0 + n_], in_=ps[bks[bi]][:, 0:n_], func=AF.Identity, bias=vecs[:, V_CB + i:V_CB + i + 1],
                    scale=1.0), reads=[psr[bks[bi]], r_vecs], writes=[r_R[i]])
        kb.barrier()
        CBT = [Hf[:, 0 + q * 544:(q + 1) * 544].bitcast(BF16) for q in range(2)]
        SQ1 = [Hf[:, 1088 + q * 544:1088 + (q + 1) * 544].bitcast(BF16) for q in range(2)]
        MU = Hf[:, 2176:2176 + T]
        RS1 = Hf[:, 3264:3264 + T]
        MT = [Hf[:, 4352 + q * 544:4352 + (q + 1) * 544].bitcast(BF16) for q in range(2)]
        r_cbt, r_sq1, r_mt = [R_("cbt0"), R_("cbt1")], [R_("sq10"), R_("sq11")], [R_("mt0"), R_("mt1")]
        r_mu, r_rs1 = R_("mu"), R_("rs1")
        bk1 = [bank() for _ in TBS]
        bk2 = [bank() for _ in TBS]
        for i in range(16):
            q = i % 2
            kb.op("act", lambda a, i=i, q=q: a.activation(out=CBT[q], in_=Rb[:, i, :], func=AF.Copy), reads=[r_R[i]],
                  writes=[r_cbt[q]])
            kb.op("act", lambda a, i=i, q=q: a.activation(out=SQ1[q], in_=Rb[:, i, :], func=AF.Square), reads=[r_R[i]],
                  writes=[r_sq1[q]])

            def mmst(pe, i=i, q=q):
                ins = None
                for bi, (t0, tn) in enumerate(TBS):
                    ins = pe.matmul(ps[bk1[bi]][:, 0:tn], lhsT=ones[:], rhs=CBT[q][:, t0:t0 + tn], start=(i == 0), stop=(i == 15))
                    ins = pe.matmul(ps[bk2[bi]][:, 0:tn], lhsT=ones[:], rhs=SQ1[q][:, t0:t0 + tn], start=(i == 0), stop=(i == 15))
                return ins
            kb.op("pe", mmst, reads=[r_cbt[q], r_sq1[q], R_("ones")], writes=[psr[b] for b in bk1 + bk2])
        for bi, (t0, tn) in enumerate(TBS):
            kb.op("dve", lambda v, bi=bi, t0=t0, tn=tn: v.tensor_scalar(out=MU[:, t0:t0 + tn], in0=ps[bk1[bi]][:, 0:tn],
                                                                       scalar1=1.0 / D, scalar2=None, op0=ALU.mult),
                  reads=[psr[bk1[bi]]], writes=[r_mu])
            kb.op("dve", lambda v, bi=bi, t0=t0, tn=tn: v.tensor_scalar(out=RS1[:, t0:t0 + tn], in0=ps[bk2[bi]][:, 0:tn],
                                                                       scalar1=1.0 / D, scalar2=EPS, op0=ALU.mult, op1=ALU.add),
                  reads=[psr[bk2[bi]]], writes=[r_rs1])
        r_m2 = R_("mu2")
        MU2 = Hf[:, 5440:5440 + T]
        kb.op("dve", lambda v: v.tensor_tensor(out=MU2, in0=MU, in1=MU, op=ALU.mult), reads=[r_mu], writes=[r_m2])
        kb.op("dve", lambda v: v.tensor_tensor(out=RS1, in0=RS1, in1=MU2, op=ALU.subtract), reads=[r_rs1, r_m2], writes=[r_rs1])
        rsqrt_ip(RS1, r_rs1)
        for i in range(16):
            q = i % 2
            kb.op("dve", lambda v, i=i: v.tensor_tensor(out=Rb[:, i, :], in0=Rb[:, i, :], in1=MU, op=ALU.subtract),
                  reads=[r_R[i], r_mu], writes=[r_R[i]])
            kb.op("dve", lambda v, i=i: v.tensor_tensor(out=Rb[:, i, :], in0=Rb[:, i, :], in1=RS1, op=ALU.mult),
                  reads=[r_R[i], r_rs1], writes=[r_R[i]])
            kb.op("act", lambda a, i=i, q=q: a.activation(out=MT[q], in_=Rb[:, i, :], func=AF.Silu,
                                                         bias=vecs[:, V_LNB + i:V_LNB + i + 1],
                                                         scale=vecs[:, V_LNG + i:V_LNG + i + 1]),
                  reads=[r_R[i], r_vecs], writes=[r_mt[q]])
            kb.op("pool" if i % 2 == 0 else "dve",
                  lambda g, i=i, q=q: g.tensor_tensor(out=SZ[:, i, :], in0=SZ[:, i, :], in1=MT[q], op=ALU.mult),
                  reads=[r_mt[q], r_SZ[i]], writes=[r_SZ[i]])
        kb.barrier()
        XSTG = [Hf[:, 6528 + i * T:6528 + (i + 1) * T] for i in range(2)]
        r_xstg = [R_("xstg2_0"), R_("xstg2_1")]
        outproj(56, x1_d, r_R, False, SZ, r_SZ, XSTG, r_xstg, r_H)
        kb.barrier()
        bks = [bank() for _ in TBS]
        for k in range(KC):
            sq = SQT[k % 2]
            rsq = R_("sqt%d" % (k % 2))
            kb.op("act", lambda a, k=k, sq=sq: a.activation(out=sq, in_=Rb[:, k, :], func=AF.Square), reads=[r_R[k]],
                  writes=[rsq])

            def mmf(pe, k=k, sq=sq):
                ins = None
                for bi, (t0, tn) in enumerate(TBS):
                    ins = pe.matmul(ps[bks[bi]][:, 0:tn], lhsT=ones[:], rhs=sq[:, t0:t0 + tn], start=(k == 0), stop=(k == KC - 1))
                return ins
            kb.op("pe", mmf, reads=[rsq, R_("ones")], writes=[psr[b] for b in bks])
        r_rstd = R_("rstd")
        for bi, (t0, tn) in enumerate(TBS):
            kb.op("dve", lambda v, bi=bi, t0=t0, tn=tn: v.tensor_scalar(out=RSTD[:, t0:t0 + tn], in0=ps[bks[bi]][:, 0:tn],
                                                                       scalar1=1.0 / D, scalar2=EPS, op0=ALU.mult, op1=ALU.add),
                  reads=[psr[bks[bi]]], writes=[r_rstd])
        rsqrt_ip(RSTD, r_rstd)
        OST = [Hf[:, i * T:(i + 1) * T] for i in range(2)]
        r_ost = [R_("ost0"), R_("ost1")]
        for k in range(KC):
            q = k % 2
            kb.op("dve", lambda v, k=k, q=q: v.scalar_tensor_tensor(out=OST[q], in0=Rb[:, k, :], scalar=vecs[:, V_FG + k:V_FG + k + 1],
                                                                   in1=RSTD, op0=ALU.mult, op1=ALU.mult),
                  reads=[r_R[k], r_rstd, r_vecs], writes=[r_ost[q]])
            kb.final.append(kb.dma("sp", yT_d[k * 128:(k + 1) * 128, 0:NP_], OST[q][:, 0:NP_], reads=[r_ost[q]],
                                   writes=[R_("yT%d" % k)], store=True))
            kb.final.append(kb.dma("sp", yT_d[k * 128:(k + 1) * 128, NP_:T], OST[q][:, NP_:T], reads=[r_ost[q]],
                                   writes=[R_("yTs%d" % k)], store=True))


    kb.op("pool", lambda g: g.memset(ust[:], 0.0), writes=[R_("ust")])
    kb.op("pool", lambda g: g.memset(a1T, 0.0), writes=[R_("a1T")])
    kb.op("pool", lambda g: g.memset(KDTMP2, 0.0), writes=[R_("kdb1")])
    wprefetch(3)
    for PS in range(NPASS):
        run_pass(PS)

    e_sp = kb.engs["sp"]
    fw = {}
    for s, v in kb.final:
        fw[s] = max(fw.get(s, 0), v)
    e_sp.prog.append((list(fw.items()), None, None, 0))

    with nc.Block() as block:
        @block.sync
        def _(en):
            kb.emit("sp", en)

        @block.gpsimd
        def _(en):
            kb.emit("pool", en)

        @block.scalar
        def _(en):
            kb.emit("act", en)

        @block.vector
        def _(en):
            kb.emit("dve", en)

        @block.tensor
        def _(en):
            kb.emit("pe", en)
    return nc


def _prep_inputs(inp):
    f = np.float32
    g = lambda k: np.asarray(inp[k], dtype=f)
    xp, xs = g("x_prompt"), g("x_sample")
    w_in0 = g("gla_w_in")[0]
    w_out0 = g("gla_w_out")[0]
    w_in1 = g("conv_w_in")[0]
    w_out1 = g("conv_w_out")[0]

    def slotify(cols):
        return np.ascontiguousarray(cols.reshape(KC, 128, 256).transpose(1, 0, 2))

    w0 = np.empty((32, 128, KC, 256), f)
    for j in range(8):
        w0[j] = slotify(np.concatenate([w_in0[:, 1024 + 128 * j:1024 + 128 * (j + 1)], w_in0[:, 128 * j:128 * (j + 1)]], 1))
    for s in range(8):
        w0[8 + s] = slotify(w_in0[:, 2048 + 256 * s:2048 + 256 * (s + 1)])
        w0[16 + s] = slotify(w_in0[:, 4096 + 256 * s:4096 + 256 * (s + 1)])
        w0[24 + s] = slotify(w_out0[:, 256 * s:256 * (s + 1)])
    w1 = np.empty((32, 128, KC, 256), f)
    for i in range(16):
        w1[i] = slotify(np.concatenate([w_in1[:, 128 * i:128 * (i + 1)], w_in1[:, 2048 + 128 * i:2048 + 128 * (i + 1)]], 1))
    for s in range(8):
        w1[16 + s] = slotify(w_in1[:, 4096 + 256 * s:4096 + 256 * (s + 1)])
        w1[24 + s] = slotify(w_out1[:, 256 * s:256 * (s + 1)])
    wa1 = np.ascontiguousarray(w_in0[:, 6144:6160].reshape(KC, 128, 16).transpose(1, 0, 2))
    wa2 = np.ascontiguousarray(g("gla_w_a2")[0])

    def pv(v):
        return v.reshape(-1, 128).T

    vecs = np.concatenate([
        pv(g("norm_g")[0]), pv(g("norm_g")[1]), pv(g("final_norm_g")), pv(g("gla_norm_g")[0]), pv(g("gla_b_a")[0]),
        pv(g("conv_b_in")[0]), pv(g("conv_b")[0]), pv(g("conv_ln_g")[0]), pv(g("conv_ln_b")[0]),
        g("conv_w")[0].reshape(31, 16, 128).transpose(2, 1, 0).reshape(128, 496),
    ], axis=1).astype(f)
    assert vecs.shape == (128, 664)
    vecs = np.ascontiguousarray(vecs)
    sgl = g("state_gla")[0]
    scv = g("state_conv")[0]
    maps = []
    zblk = np.zeros((1024, D), f)
    for c in range(8):
        b, seg = c // 4, c % 4
        blks = []
        for i in range(seg - 3, seg + 1):
            blks.append(xp[b, i * 1024:(i + 1) * 1024, :] if i >= 0 else zblk)
        xT = np.ascontiguousarray(np.concatenate(blks + [xs[c]], 0).T)
        sg = np.ascontiguousarray(sgl[c].reshape(4, 2, 128, 512).transpose(2, 0, 1, 3).reshape(128, 8, 512))
        sc = np.ascontiguousarray(scv[c].reshape(30, 16, 128).transpose(2, 1, 0))
        flag = np.full((128, 1), 1.0 if seg > 0 else 0.0, f)
        maps.append({"xT": xT, "w0": w0, "w1": w1, "wa1": wa1, "wa2": wa2, "vecs": vecs, "sg": sg, "sc": sc,
                     "flag": flag})
    return maps


_NC = None


def kernel(**inputs):
    global _NC
    maps = _prep_inputs(inputs)
    if _NC is None:
        _NC = build_nc()
    res = run_bass_kernel_spmd(_NC, maps, core_ids=list(range(8)))
    R = res.results
    f = np.float32
    y_prompt = np.empty((2, 4096, D), f)
    y_sample = np.empty((8, 64, D), f)
    gla_p = np.empty((1, 2, 4, 256, 512), f)
    conv_p = np.empty((1, 2, 30, D), f)
    gla_s = np.empty((1, 8, 4, 256, 512), f)
    conv_s = np.empty((1, 8, 30, D), f)

    def unS(a):
        return a.transpose(1, 0, 2).reshape(4, 256, 512)

    for c in range(8):
        b, seg = c // 4, c % 4
        yT = R[c]["yT"]
        y_prompt[b, seg * 1024:(seg + 1) * 1024, :] = yT[:, :1024].T
        y_sample[c] = yT[:, 1024:].T
        gla_s[0, c] = unS(R[c]["glas"])
        cst = R[c]["cst"]
        conv_s[0, c] = cst[:, :, 30:60].transpose(2, 1, 0).reshape(30, D)
        if seg == 3:
            gla_p[0, b] = unS(R[c]["glap"])
            conv_p[0, b] = cst[:, :, 0:30].transpose(2, 1, 0).reshape(30, D)
    return (y_prompt, y_sample, gla_p, conv_p, gla_s, conv_s)
```
